# Optimizing a Trainium2 kernel written in Bass

```python
import math
import jax, jax.numpy as jnp
from jax import lax
import numpy as np

D_MODEL = 2048
BATCH = 4
SEQ = 4096
DEPTH = 2

MIX = D_MODEL
D_MLA = MIX // 2
D_CONV = MIX - D_MLA
N_HEADS = 8
NOPE_DIM = 128
ROPE_DIM = 64
V_DIM = D_MLA // N_HEADS
QK_DIM = NOPE_DIM + ROPE_DIM
Q_LORA = 512
KV_LORA = 256
ROPE_THETA = 10000.0
Q_BLOCK = 128
CONV_K = 31
IN_COLS = Q_LORA + KV_LORA + ROPE_DIM + D_MLA + 2 * D_CONV + D_CONV
EPS = 1e-6

kernel_name = "hybrid_mla_conformer_conv_headgroups"


def rms_norm(x, g):
    xf = x.astype(jnp.float32)
    y = xf * lax.rsqrt(jnp.mean(xf * xf, axis=-1, keepdims=True) + EPS)
    return (y * g.astype(jnp.float32)).astype(x.dtype)


def layer_norm(x, g, b):
    xf = x.astype(jnp.float32)
    mu = jnp.mean(xf, axis=-1, keepdims=True)
    var = jnp.mean(jnp.square(xf - mu), axis=-1, keepdims=True)
    y = (xf - mu) * lax.rsqrt(var + EPS)
    return (y * g.astype(jnp.float32) + b.astype(jnp.float32)).astype(x.dtype)


def rope_tables(positions, dtype):
    inv_freq = 1.0 / (ROPE_THETA ** (jnp.arange(0, ROPE_DIM, 2, dtype=jnp.float32) / ROPE_DIM))
    ang = positions.astype(jnp.float32)[..., None] * inv_freq
    return jnp.cos(ang)[:, :, None, :].astype(dtype), jnp.sin(ang)[:, :, None, :].astype(dtype)


def apply_rope(x, cos, sin):
    x1, x2 = jnp.split(x, 2, axis=-1)
    return jnp.concatenate([x1 * cos - x2 * sin, x2 * cos + x1 * sin], axis=-1)


def causal_block_attention(q, k, v):
    b, s, h, dq = q.shape
    nb = s // Q_BLOCK
    scale = 1.0 / math.sqrt(dq)
    qb = q.reshape(b, nb, Q_BLOCK, h, dq).transpose(1, 0, 2, 3, 4)
    key_pos = jnp.arange(s)

    def one_block(args):
        q_i, blk = args
        scores = jnp.einsum('bqhd,bkhd->bhqk', q_i, k).astype(jnp.float32) * scale
        q_pos = blk * Q_BLOCK + jnp.arange(Q_BLOCK)
        mask = key_pos[None, :] <= q_pos[:, None]
        scores = jnp.where(mask[None, None], scores, -jnp.inf)
        p = jax.nn.softmax(scores, axis=-1).astype(v.dtype)
        return jnp.einsum('bhqk,bkhd->bqhd', p, v)

    out = lax.map(one_block, (qb, jnp.arange(nb)))
    return out.transpose(1, 0, 2, 3, 4).reshape(b, s, h, v.shape[-1])


def causal_depthwise_conv(u, w, bias):
    out = lax.conv_general_dilated(
        u, w[:, None, :].astype(u.dtype),
        window_strides=(1,), padding=((CONV_K - 1, 0),),
        dimension_numbers=('NWC', 'WIO', 'NWC'),
        feature_group_count=u.shape[-1])
    return out + bias


def setup_inputs(seed: int = 0) -> dict:
    key = jax.random.key(seed)
    ks = jax.random.split(key, 24)
    f32 = jnp.float32

    def w(k, shape, fan_in):
        return jax.random.normal(k, shape, f32) * (fan_in ** -0.5)

    def gain(k, shape):
        return 1.0 + 0.05 * jax.random.normal(k, shape, f32)

    def small(k, shape):
        return 0.02 * jax.random.normal(k, shape, f32)

    x = jax.random.normal(ks[0], (BATCH, SEQ, D_MODEL), f32)
    c = jax.random.normal(ks[1], (BATCH, D_MODEL), f32)
    offsets = jax.random.randint(ks[2], (BATCH, 1), 0, 1024, dtype=jnp.int32)
    positions = offsets + jnp.arange(SEQ, dtype=jnp.int32)[None, :]
    return {
        'x': x,
        'c': c,
        'positions': positions,
        'ada_w': w(ks[3], (DEPTH, D_MODEL, 3 * D_MODEL), D_MODEL),
        'ada_b': small(ks[4], (DEPTH, 3 * D_MODEL)),
        'norm_g': gain(ks[5], (DEPTH, D_MODEL)),
        'w_in': w(ks[6], (DEPTH, D_MODEL, IN_COLS), D_MODEL),
        'q_lat_g': gain(ks[7], (DEPTH, Q_LORA)),
        'w_q_up': w(ks[8], (DEPTH, Q_LORA, N_HEADS * QK_DIM), Q_LORA),
        'kv_lat_g': gain(ks[9], (DEPTH, KV_LORA)),
        'w_kv_up': w(ks[10], (DEPTH, KV_LORA, N_HEADS * (NOPE_DIM + V_DIM)), KV_LORA),
        'q_norm_g': gain(ks[11], (DEPTH, QK_DIM)),
        'k_norm_g': gain(ks[12], (DEPTH, QK_DIM)),
        'glu_b': small(ks[13], (DEPTH, 2 * D_CONV)),
        'dw_w': w(ks[14], (DEPTH, CONV_K, D_CONV), CONV_K),
        'dw_b': small(ks[15], (DEPTH, D_CONV)),
        'conv_ln_g': gain(ks[16], (DEPTH, D_CONV)),
        'conv_ln_b': small(ks[17], (DEPTH, D_CONV)),
        'w_pw': w(ks[18], (DEPTH, D_CONV, D_CONV), D_CONV),
        'b_pw': small(ks[19], (DEPTH, D_CONV)),
        'w_out': w(ks[20], (DEPTH, MIX, D_MODEL), MIX),
    }


def reference(x, c, positions, ada_w, ada_b, norm_g, w_in, q_lat_g, w_q_up, kv_lat_g,
              w_kv_up, q_norm_g, k_norm_g, glu_b, dw_w, dw_b, conv_ln_g, conv_ln_b,
              w_pw, b_pw, w_out):
    b, s, _ = x.shape
    cos, sin = rope_tables(positions, x.dtype)
    c_act = jax.nn.silu(c)
    splits = np.cumsum([Q_LORA, KV_LORA, ROPE_DIM, D_MLA, 2 * D_CONV]).tolist()

    for l in range(DEPTH):
        mod = c_act @ ada_w[l] + ada_b[l]
        shift, scale, gate = [m[:, None, :] for m in jnp.split(mod, 3, axis=-1)]
        h = rms_norm(x, norm_g[l]) * (1.0 + scale) + shift

        z = h @ w_in[l]
        q_lat, kv_lat, k_rope, mla_gate, conv_in, conv_gate = jnp.split(z, splits, axis=-1)

        q = (rms_norm(q_lat, q_lat_g[l]) @ w_q_up[l]).reshape(b, s, N_HEADS, QK_DIM)
        kv = (rms_norm(kv_lat, kv_lat_g[l]) @ w_kv_up[l]).reshape(b, s, N_HEADS, NOPE_DIM + V_DIM)
        k_nope, v = kv[..., :NOPE_DIM], kv[..., NOPE_DIM:]
        k_rope_h = jnp.broadcast_to(k_rope[:, :, None, :], (b, s, N_HEADS, ROPE_DIM))
        k = jnp.concatenate([k_nope, k_rope_h], axis=-1)
        q = rms_norm(q, q_norm_g[l])
        k = rms_norm(k, k_norm_g[l])
        q = jnp.concatenate([q[..., :NOPE_DIM], apply_rope(q[..., NOPE_DIM:], cos, sin)], axis=-1)
        k = jnp.concatenate([k[..., :NOPE_DIM], apply_rope(k[..., NOPE_DIM:], cos, sin)], axis=-1)
        attn = causal_block_attention(q, k, v).reshape(b, s, D_MLA)
        mla_out = attn * jax.nn.silu(mla_gate)

        u_val, u_gate = jnp.split(conv_in + glu_b[l], 2, axis=-1)
        u = u_val * jax.nn.sigmoid(u_gate)
        u = causal_depthwise_conv(u, dw_w[l], dw_b[l])
        u = jax.nn.silu(layer_norm(u, conv_ln_g[l], conv_ln_b[l]))
        u = u @ w_pw[l] + b_pw[l]
        conv_out = u * jax.nn.silu(conv_gate)

        y = jnp.concatenate([mla_out, conv_out], axis=-1) @ w_out[l]
        x = x + gate * y
    return x
```

```python
import numpy as np
import ml_dtypes
import concourse.bass as bass
import concourse.mybir as mybir
from concourse.bass_utils import run_bass_kernel_spmd

F32 = mybir.dt.float32
BF16 = mybir.dt.bfloat16
I32 = mybir.dt.int32
ALU = mybir.AluOpType
AF = mybir.ActivationFunctionType
AX = mybir.AxisListType

ENGS = ("pe", "act", "dve", "pool", "sp")


class Tile:
    __slots__ = ("name", "ap", "last_w", "reads", "sem")

    def __init__(self, name, ap):
        self.name = name
        self.ap = ap
        self.last_w = None
        self.reads = []
        self.sem = None

    def __getitem__(self, k):
        return self.ap[k]


class Grp:
    __slots__ = ("sem", "final", "closed")


class Op:
    __slots__ = ("eng", "fn", "deps", "signal", "val", "grp", "inc", "seq")


class Prog:
    def __init__(self, nc, arena_f32=51712, n_dma_sems=64):
        self.nc = nc
        self.ops = {e: [] for e in ENGS}
        self.nsig = {e: 0 for e in ENGS}
        self.esem = {e: nc.alloc_semaphore("sem_" + e) for e in ENGS}
        self.dsems = [nc.alloc_semaphore("dsem%d" % i) for i in range(n_dma_sems)]
        self.dsem_cum = [0] * n_dma_sems
        self.dsem_grp = [None] * n_dma_sems
        self.dsem_free = list(range(n_dma_sems))
        self.arena = nc.alloc_sbuf_tensor("arena", [128, arena_f32], F32)
        self.arena_n = arena_f32
        self.top = 0
        self.psum = nc.alloc_psum_tensor("psum_all", [128, 8 * 512], F32)
        self.seq = 0
        self.live_tiles = []
        self.ndram = 0

    def sb(self, name, shape, dtype=F32):
        esz = 4 if dtype in (F32, I32) else 2
        free = int(np.prod(shape[1:]))
        nf32 = (free * esz + 3) // 4
        nf32 = (nf32 + 7) // 8 * 8
        assert self.top + nf32 <= self.arena_n, "SBUF arena overflow at %s (%d + %d)" % (name, self.top, nf32)
        ap = self.arena.ap()[0:shape[0], self.top:self.top + nf32]
        self.top += nf32
        if dtype != F32:
            ap = ap.bitcast(dtype)
        ap = ap[:, 0:free]
        if len(shape) > 2:
            names = " ".join("d%d" % i for i in range(len(shape) - 1))
            kw = {"d%d" % i: shape[i + 1] for i in range(len(shape) - 1)}
            ap = ap.rearrange("p (%s) -> p %s" % (names, names), **kw)
        t = Tile(name, ap)
        self.live_tiles.append(t)
        return t

    def mark(self):
        return (self.top, len(self.live_tiles))

    def release(self, mark):
        top, n = mark
        for t in self.live_tiles[n:]:
            if t.sem is not None:
                self.dsem_free.append(t.sem)
                t.sem = None
        del self.live_tiles[n:]
        self.top = top

    def ps(self, name, bank, off=0, n=512, dtype=F32, parts=128):
        ap = self.psum.ap()[0:parts, bank * 512 + off: bank * 512 + off + n]
        if dtype != F32:
            ap = ap.bitcast(dtype)
        return Tile(name, ap)

    def psv(self, name, bank0, nwords, parts=128):
        return Tile(name, self.psum.ap()[0:parts, bank0 * 512: bank0 * 512 + nwords])

    def dram(self, name, shape, dtype, kind="Internal"):
        t = self.nc.dram_tensor(name, list(shape), dtype, kind=kind)
        return Tile(name, t.ap())

    def _mk(self, eng, fn, reads, writes):
        op = Op()
        op.eng = eng
        op.fn = fn
        op.deps = []
        op.signal = False
        op.val = None
        op.grp = None
        op.inc = None
        self.seq += 1
        op.seq = self.seq
        deps = op.deps
        for t in reads:
            if t.last_w is not None:
                deps.append(t.last_w)
        for t in writes:
            if t.last_w is not None:
                deps.append(t.last_w)
            deps.extend(t.reads)
        for d in deps:
            if d.grp is not None:
                d.grp.closed = True
        return op

    def _commit(self, op, reads, writes):
        for t in writes:
            t.last_w = op
            t.reads = []
        for t in reads:
            t.reads.append(op)
        self.ops[op.eng].append(op)

    def op(self, eng, fn, reads=(), writes=()):
        op = self._mk(eng, fn, reads, writes)
        self._commit(op, reads, writes)
        return op

    def _sem_for(self, tile):
        if tile.sem is None:
            assert self.dsem_free, "out of DMA semaphores"
            tile.sem = self.dsem_free.pop(0)
        return tile.sem

    def dma(self, q, out, in_, reads, writes, semtile, inc=16, fn=None):
        si = self._sem_for(semtile)
        if fn is None:
            def fn(e, out=out, in_=in_):
                return e.dma_start(out=out, in_=in_)
        op = self._mk(q, fn, reads, writes)
        g = self.dsem_grp[si]
        if g is None or g.closed:
            if g is not None:
                op.deps.append(("grp", g))
            g = Grp()
            g.sem = si
            g.closed = False
            g.final = self.dsem_cum[si]
            self.dsem_grp[si] = g
        self.dsem_cum[si] += inc
        g.final = self.dsem_cum[si]
        op.grp = g
        op.inc = inc
        self._commit(op, reads, writes)
        return op

    def barrier(self):
        lasts = []
        for e in ENGS:
            if self.ops[e]:
                lasts.append(self.ops[e][-1])
        grps = [("grp", g) for g in self.dsem_grp if g is not None]
        for _, g in grps:
            g.closed = True
        for e in ENGS:
            op = Op()
            op.eng = e
            op.fn = None
            op.deps = list(lasts) + list(grps)
            op.signal = False
            op.val = None
            op.grp = None
            op.inc = None
            self.seq += 1
            op.seq = self.seq
            self.ops[e].append(op)
        for t in self.live_tiles:
            t.last_w = None
            t.reads = []

    def finalize(self):
        nc = self.nc
        for e in ENGS:
            for op in self.ops[e]:
                for d in op.deps:
                    if isinstance(d, tuple):
                        d[1].closed = True
                    elif d.grp is not None:
                        d.grp.closed = True
                    else:
                        d.signal = True
        for e in ENGS:
            cnt = 0
            for op in self.ops[e]:
                if op.grp is None and op.signal and op.fn is not None:
                    cnt += 1
                    op.val = cnt
        ops = self.ops
        esem = self.esem
        dsems = self.dsems

        def replay(eng_name, e):
            seen = {}
            for op in ops[eng_name]:
                waits = {}
                for d in op.deps:
                    if isinstance(d, tuple):
                        g = d[1]
                        key = ("d", g.sem)
                        v = g.final
                    elif d.grp is not None:
                        key = ("d", d.grp.sem)
                        v = d.grp.final
                    else:
                        if d.fn is None:
                            continue
                        if d.eng == eng_name and eng_name in ("pe", "sp"):
                            continue
                        key = ("e", d.eng)
                        v = d.val
                    if waits.get(key, 0) < v:
                        waits[key] = v
                for key, v in waits.items():
                    if seen.get(key, 0) >= v:
                        continue
                    seen[key] = v
                    sem = dsems[key[1]] if key[0] == "d" else esem[key[1]]
                    e.wait_ge(sem, v)
                if op.fn is None:
                    continue
                ins = op.fn(e)
                if op.grp is not None:
                    ins.then_inc(dsems[op.grp.sem], op.inc)
                elif op.signal:
                    ins.then_inc(esem[eng_name], 1)

        with nc.Block() as block:
            block.tensor(lambda e: replay("pe", e))
            block.scalar(lambda e: replay("act", e))
            block.vector(lambda e: replay("dve", e))
            block.gpsimd(lambda e: replay("pool", e))
            block.sync(lambda e: replay("sp", e))


def bc(ap, shape):
    return ap.broadcast_to(list(shape))


def E(P, eng, method, reads, writes, **kw):
    return P.op(eng, lambda e: getattr(e, method)(**kw), reads, writes)


D = 2048
NT = 2048
NTT = 16
NQT = 4
NO_UTAIL = 0
EPS = 1e-6
C_QL, C_KV, C_KR, C_MG, C_UV, C_UG, C_CG = 0, 512, 768, 832, 1856, 2880, 3904
TILES_A = (0, 3, 4, 7)
TILES_B = (1, 2, 5, 6)
KVROWS = 2560


class Consts:
    pass


def emit_consts(P):
    C = Consts()
    C.identf = P.sb("identf", [128, 128])
    C.ident = P.sb("ident", [128, 128], BF16)
    C.ones = P.sb("ones", [128, 128], BF16)
    C.eps = P.sb("eps", [128, 1])
    E(P, "pool", "memset", [], [C.identf], ap=C.identf[:, :], constant=0.0)
    E(P, "pool", "affine_select", [C.identf], [C.identf], out=C.identf[:, :], in_=C.identf[:, :],
      pattern=[[-1, 128]], compare_op=ALU.not_equal, fill=1.0, base=0, channel_multiplier=1)
    E(P, "dve", "tensor_copy", [C.identf], [C.ident], out=C.ident[:, :], in_=C.identf[:, :])
    E(P, "pool", "memset", [], [C.ones], ap=C.ones[:, :], constant=1.0)
    E(P, "pool", "memset", [], [C.eps], ap=C.eps[:, :], constant=EPS)
    return C


def rstd_op(P, C, out, ss, n):
    E(P, "act", "activation", [ss, C.eps], [out], out=out[:, :], in_=ss[:, :], func=AF.Sqrt,
      scale=1.0 / n, bias=C.eps[:, 0:1])
    E(P, "dve", "reciprocal", [out], [out], out=out[:, :], in_=out[:, :])


class WLoader:
    def __init__(self, P, nstage=4, words=2048, q="sp"):
        self.P = P
        self.stg = [P.sb("wstg%d" % i, [128, words]) for i in range(nstage)]
        self.words = words
        self.i = 0
        self.ci = 0
        self.q = q
        self.engs = ("pool", "dve", "act")

    def load(self, dst, dst_view, src_tile, src_ap, shape, rowscale=None, colscale=None, engs=None):
        P = self.P
        a, n = shape[1], shape[2]
        assert a * n <= self.words
        st = self.stg[self.i % len(self.stg)]
        self.i += 1
        sv = st[:, 0:a * n].rearrange("p (a n) -> p a n", a=a)
        P.dma(self.q, sv, src_ap, [src_tile], [st], st)
        engs = engs or self.engs
        if rowscale is not None:
            rt, rap = rowscale
            for j in range(a):
                eng = ("dve", "act")[self.ci % 2]
                self.ci += 1
                if eng == "dve":
                    E(P, "dve", "tensor_scalar", [st, rt], [dst], out=dst_view[:, j, :], in0=sv[:, j, :],
                      scalar1=rap[:, j:j + 1], scalar2=None, op0=ALU.mult)
                else:
                    E(P, "act", "activation", [st, rt], [dst], out=dst_view[:, j, :], in_=sv[:, j, :],
                      func=AF.Copy, scale=rap[:, j:j + 1])
        elif colscale is not None:
            ct, cap = colscale
            eng = ("dve", "pool")[self.ci % 2]
            self.ci += 1
            E(P, eng, "tensor_tensor", [st, ct], [dst], out=dst_view, in0=sv,
              in1=bc(cap.unsqueeze(1), [128, a, n]), op=ALU.mult)
        else:
            eng = engs[self.ci % len(engs)]
            self.ci += 1
            if eng == "act":
                E(P, "act", "activation", [st], [dst], out=dst_view, in_=sv, func=AF.Copy)
            else:
                E(P, eng, "tensor_copy", [st], [dst], out=dst_view, in_=sv)


def load_pp(P, C, pieces, name, pbank=7):
    rows = sum(ap.shape[0] for _, ap in pieces)
    assert rows <= 128
    out = P.sb(name, [128, rows])
    stk = P.sb(name + "_stk", [128, 128])
    r = 0
    for dt, ap in pieces:
        n = ap.shape[0]
        P.dma("sp", stk[r:r + n, :], ap, [dt], [stk], stk)
        r += n
    pt = P.ps(name + "_pt", pbank, 0, 128)
    E(P, "pe", "transpose", [stk, C.identf], [pt], out=pt[:, 0:rows], in_=stk[0:rows, :], identity=C.identf[0:rows, 0:rows])
    E(P, "dve", "tensor_copy", [pt], [out], out=out[:, :], in_=pt[:, 0:rows])
    return out


def sumsq(P, C, ps_ap, ps_tile, junk, ss):
    E(P, "pool", "memset", [], [ss], ap=ss[:, :], constant=0.0)
    E(P, "act", "activation", [ps_tile, ss], [junk, ss], out=junk[:, 0:ps_ap.shape[1]], in_=ps_ap, func=AF.Square,
      accum_out=ss[:, 0:1])


def emit_ph1a(P, C, io, stop_after=99):
    m0 = P.mark()
    x, wi = io["x"], io["w_in"]
    pp = load_pp(P, C, [(io["mod"], io["mod"].ap.rearrange("(c p) -> c p", p=128)),
                        (io["norm_g"], io["norm_g"].ap.rearrange("(c p) -> c p", p=128)),
                        (io["glu_b"], io["glu_b"].ap.rearrange("(c p) -> c p", p=128)),
                        (io["q_lat_g"], io["q_lat_g"].ap.rearrange("(c p) -> c p", p=128)),
                        (io["kv_lat_g"], io["kv_lat_g"].ap.rearrange("(c p) -> c p", p=128))], "pp")
    SH, SC, NG, GB, QLG, KLG = 0, 16, 48, 64, 80, 84
    gm = P.sb("gm", [128, 16])
    cos = P.sb("cos", [128, 16, 32])
    sin = P.sb("sin", [128, 16, 32])
    gq = P.sb("gq", [128, 192])
    gk = P.sb("gk", [128, 192])
    hT = P.sb("hT", [128, 16, NT], BF16)
    WL = WLoader(P, nstage=4, words=2048)
    w1 = P.sb("w1", [128, 16, 832], BF16)
    wq = P.sb("wq", [128, 4, 1536], BF16)
    wkv = P.sb("wkv", [128, 2, 2048], BF16)
    ms0 = P.mark()
    E(P, "dve", "scalar_tensor_tensor", [pp], [gm], out=gm[:, :], in0=pp[:, SC:SC + 16], scalar=1.0,
      in1=pp[:, NG:NG + 16], op0=ALU.add, op1=ALU.mult)
    posi = P.sb("posi", [128, 128], I32)
    posf = P.sb("posf", [128, 128])
    P.dma("sp", posi[0:16, :], io["pos"].ap.rearrange("(c p) -> c p", p=128), [io["pos"]], [posi], posi)
    E(P, "dve", "tensor_copy", [posi], [posf], out=posf[0:16, :], in_=posi[0:16, :])
    ptp = P.ps("pos_pt", 6, 0, 128)
    E(P, "pe", "transpose", [posf, C.identf], [ptp], out=ptp[:, 0:16], in_=posf[0:16, :], identity=C.identf[0:16, 0:16])
    posT = P.sb("posT", [128, 16])
    E(P, "dve", "tensor_copy", [ptp], [posT], out=posT[:, :], in_=ptp[:, 0:16])
    invf = P.sb("invf", [128, 32])
    P.dma("sp", invf[:, :], io["invf"].ap.partition_broadcast(128), [io["invf"]], [invf], invf)
    ang = P.sb("ang", [128, 16, 32])
    angi = P.sb("angi", [128, 16, 32], I32)
    angf = P.sb("angf", [128, 16, 32])
    angm = P.sb("angm", [128, 16, 32])
    E(P, "dve", "tensor_tensor", [posT, invf], [ang], out=ang[:, :, :], in0=bc(posT[:, :].unsqueeze(2), [128, 16, 32]),
      in1=bc(invf[:, :].unsqueeze(1), [128, 16, 32]), op=ALU.mult)
    for dst, off in ((sin, 0.0), (cos, 0.25)):
        E(P, "dve", "tensor_scalar", [ang], [angm], out=angm[:, :, :], in0=ang[:, :, :], scalar1=1.0 / (2 * np.pi),
          scalar2=off, op0=ALU.mult, op1=ALU.add)
        E(P, "dve", "tensor_copy", [angm], [angi], out=angi[:, :, :], in_=angm[:, :, :])
        E(P, "dve", "tensor_copy", [angi], [angf], out=angf[:, :, :], in_=angi[:, :, :])
        E(P, "dve", "tensor_tensor", [angm, angf], [angm], out=angm[:, :, :], in0=angm[:, :, :], in1=angf[:, :, :], op=ALU.subtract)
        E(P, "dve", "tensor_single_scalar", [angm], [angf], out=angf[:, :, :], in_=angm[:, :, :], scalar=0.5, op=ALU.is_gt)
        E(P, "dve", "tensor_tensor", [angm, angf], [angm], out=angm[:, :, :], in0=angm[:, :, :], in1=angf[:, :, :], op=ALU.subtract)
        E(P, "act", "activation", [angm], [dst], out=dst[:, :, :], in_=angm[:, :, :], func=AF.Sin, scale=float(2 * np.pi))
    P.dma("sp", gq[:, :], io["q_norm_g"].ap.partition_broadcast(128), [io["q_norm_g"]], [gq], gq)
    P.dma("sp", gk[:, :], io["k_norm_g"].ap.partition_broadcast(128), [io["k_norm_g"]], [gk], gk)
    E(P, "dve", "tensor_scalar", [gq], [gq], out=gq[:, :], in0=gq[:, :], scalar1=float(192.0 ** -0.5), scalar2=None, op0=ALU.mult)

    wiv = wi.ap.rearrange("(k p) n -> p k n", p=128)
    for k in range(0, 16, 2):
        WL.load(w1, w1[:, k:k + 2, :], wi, wiv[:, k:k + 2, 0:832], [128, 2, 832])
    wqv = io["w_q_up"].ap.rearrange("(k p) (h j) -> p k h j", p=128, j=192)
    for k in range(4):
        WL.load(wq, wq[:, k:k + 1, 0:1024].rearrange("p a (h j) -> p (a h) j", j=128), io["w_q_up"],
                wqv[:, k, :, 0:128], [128, 8, 128], rowscale=None)
        WL.load(wq, wq[:, k:k + 1, 1024:1536].rearrange("p a (h j) -> p (a h) j", j=64), io["w_q_up"],
                wqv[:, k, :, 128:192], [128, 8, 64], rowscale=None)
    wkvv = io["w_kv_up"].ap.rearrange("(k p) n -> p k n", p=128)
    for k in range(2):
        WL.load(wkv, wkv[:, k:k + 1, :], io["w_kv_up"], wkvv[:, k:k + 1, :], [128, 1, 2048])
    for k in range(4):
        E(P, "dve", "tensor_scalar", [wq, pp], [wq], out=wq[:, k, :], in0=wq[:, k, :], scalar1=pp[:, QLG + k:QLG + k + 1],
          scalar2=None, op0=ALU.mult)
    for k in range(2):
        E(P, "dve", "tensor_scalar", [wkv, pp], [wkv], out=wkv[:, k, :], in0=wkv[:, k, :], scalar1=pp[:, KLG + k:KLG + k + 1],
          scalar2=None, op0=ALU.mult)

    P.barrier()
    if stop_after < 1:
        return
    P.release(ms0)
    m1 = P.mark()
    xt = [P.sb("xt%d" % i, [128, D]) for i in range(2)]
    xn = [P.sb("xn%d" % i, [128, D], BF16) for i in range(2)]
    junk = P.sb("junk", [128, D], BF16)
    ssx = [P.sb("ssx%d" % i, [128, 1]) for i in range(2)]
    rsx = [P.sb("rsx%d" % i, [128, 1]) for i in range(2)]
    ptr = [P.ps("s1pt%d" % i, i, 0, 256, BF16) for i in range(4)]
    for tt in range(NTT):
        b = tt % 2
        P.dma("sp", xt[b][:, :], x.ap[tt * 128:(tt + 1) * 128, :], [x], [xt[b]], xt[b])
        E(P, "pool", "memset", [], [ssx[b]], ap=ssx[b][:, :], constant=0.0)
        E(P, "act", "activation", [xt[b], ssx[b]], [junk, ssx[b]], out=junk[:, :], in_=xt[b][:, :], func=AF.Square,
          accum_out=ssx[b][:, 0:1])
        rstd_op(P, C, rsx[b], ssx[b], D)
        E(P, "dve", "tensor_scalar", [xt[b], rsx[b]], [xn[b]], out=xn[b][:, :], in0=xt[b][:, :], scalar1=rsx[b][:, 0:1],
          scalar2=None, op0=ALU.mult)
        for g in range(4):
            pt = ptr[g]
            for j in range(4):
                c = g * 4 + j
                E(P, "pe", "transpose", [xn[b], C.ident], [pt], out=pt[:, j * 128:(j + 1) * 128],
                  in_=xn[b][:, c * 128:(c + 1) * 128], identity=C.ident[:, :])
            for j in range(4):
                c = g * 4 + j
                if j % 2 == 0:
                    E(P, "act", "activation", [pt, gm, pp], [hT], out=hT[:, c, tt * 128:(tt + 1) * 128],
                      in_=pt[:, j * 128:(j + 1) * 128], func=AF.Identity, scale=gm[:, c:c + 1], bias=pp[:, SH + c:SH + c + 1])
                else:
                    E(P, "dve", "tensor_scalar", [pt, gm, pp], [hT], out=hT[:, c, tt * 128:(tt + 1) * 128],
                      in0=pt[:, j * 128:(j + 1) * 128], scalar1=gm[:, c:c + 1], scalar2=pp[:, SH + c:SH + c + 1],
                      op0=ALU.mult, op1=ALU.add)
    P.barrier()
    if stop_after < 2:
        return
    P.release(m1)

    m2 = P.mark()
    ps_a = P.ps("ps_a", 0)
    ps_b = P.ps("ps_b", 1)
    pt2 = P.ps("pt2", 2, 0, 512, BF16)
    pt7 = P.ps("pt7", 7, 0, 512, BF16)
    psw = [P.ps("psw%d" % i, 3 + i) for i in range(4)]
    pst = Tile("pst", pt2.ap[:, 0:512]); pst_t = pt2
    ptq = [Tile("ptq0", pt7.ap[:, 0:512]), Tile("ptq1", pt7.ap[:, 512:1024]), Tile("ptq2", pt2.ap[:, 512:1024])]
    ptq_t = [pt7, pt7, pt2]
    junkf = P.sb("junkf", [128, 512])
    ssl = P.sb("ssl", [128, 1]); rsl = P.sb("rsl", [128, 1])
    qln = P.sb("qln", [128, 512], BF16)
    qlT = P.sb("qlT", [128, 4, 128], BF16)
    sq = P.sb("sq", [128, 1536])
    ssn = P.sb("ssn", [128, 8]); ssr = P.sb("ssr", [128, 8]); rh = P.sb("rh", [128, 8])
    tmpn = P.sb("tmpn", [128, 8, 128])
    tmpr = P.sb("tmpr", [128, 8, 64])
    qr = P.sb("qr", [128, 8, 64])
    rt = [P.sb("rt%d" % i, [128, 8, 32]) for i in range(4)]
    nope_bf = P.sb("nope_bf", [128, 8, 128], BF16)
    rope_bf = P.sb("rope_bf", [128, 8, 64], BF16)
    qst = [P.sb("qst%d" % i, [128, 12, 128], BF16) for i in range(2)]
    kst = [P.sb("kst%d" % i, [128, 12, 128], BF16) for i in range(2)]
    vst = [P.sb("vst%d" % i, [128, 8, 128], BF16) for i in range(2)]
    kr = P.sb("kr", [128, 64]); krg = P.sb("krg", [128, 64]); kro = P.sb("kro", [128, 64])
    kt4 = [P.sb("kt4_%d" % i, [128, 32]) for i in range(4)]
    sskr = P.sb("sskr", [128, 1])
    kTn_d = io["kvsend"].ap[0:1024, :].rearrange("(h d) t -> d h t", d=128)
    kTr_d = io["kvsend"].ap[1024:1536, :].rearrange("(h d) t -> d h t", d=128)
    v_d = io["kvsend"].ap[1536:2560, :].rearrange("r (a c) -> (r a) c", a=2)
    qTn_d = io["qTn"].ap.rearrange("h d t -> d h t")
    qTr_d = io["qTr"].ap.rearrange("h d t -> d h t")

    def rope(src, dst_bf, tt, src_tile, nh):
        cs = bc(cos[:, tt:tt + 1, :], [128, nh, 32])
        sn = bc(sin[:, tt:tt + 1, :], [128, nh, 32])
        x1 = src[:, :, 0:32]
        x2 = src[:, :, 32:64]
        r = [t[:, 0:nh, :] for t in rt]
        E(P, "pool", "tensor_tensor", [src_tile, cos], [rt[0]], out=r[0], in0=x1, in1=cs, op=ALU.mult)
        E(P, "pool", "tensor_tensor", [src_tile, sin], [rt[1]], out=r[1], in0=x2, in1=sn, op=ALU.mult)
        E(P, "pool", "tensor_tensor", [src_tile, cos], [rt[2]], out=r[2], in0=x2, in1=cs, op=ALU.mult)
        E(P, "pool", "tensor_tensor", [src_tile, sin], [rt[3]], out=r[3], in0=x1, in1=sn, op=ALU.mult)
        return r

    for tt in range(NTT):
        ts = slice(tt * 128, (tt + 1) * 128)
        b = tt % 2
        for k in range(16):
            E(P, "pe", "matmul", [hT, w1], [ps_a], out=ps_a[:, :], lhsT=hT[:, k, ts], rhs=w1[:, k, 0:512], start=(k == 0), stop=(k == 15))
        for k in range(16):
            E(P, "pe", "matmul", [hT, w1], [ps_b], out=ps_b[:, 0:320], lhsT=hT[:, k, ts], rhs=w1[:, k, 512:832], start=(k == 0), stop=(k == 15))
        sumsq(P, C, ps_a[:, :], ps_a, junkf, ssl)
        rstd_op(P, C, rsl, ssl, 512)
        E(P, "dve", "tensor_scalar", [ps_a, rsl], [qln], out=qln[:, :], in0=ps_a[:, :], scalar1=rsl[:, 0:1], scalar2=None, op0=ALU.mult)
        for j in range(4):
            E(P, "pe", "transpose", [qln, C.ident], [pst_t], out=pst[:, j * 128:(j + 1) * 128], in_=qln[:, j * 128:(j + 1) * 128], identity=C.ident[:, :])
        E(P, "act", "activation", [pst_t], [qlT], out=qlT[:, :, :].rearrange("p a b -> p (a b)"), in_=pst[:, :], func=AF.Copy)
        for nb in range(3):
            for k in range(4):
                E(P, "pe", "matmul", [qlT, wq], [psw[nb]], out=psw[nb][:, :], lhsT=qlT[:, k, :], rhs=wq[:, k, nb * 512:(nb + 1) * 512],
                  start=(k == 0), stop=(k == 3))
        for nb in range(3):
            E(P, "act", "activation", [psw[nb]], [sq], out=sq[:, nb * 512:(nb + 1) * 512], in_=psw[nb][:, :], func=AF.Square)
        E(P, "dve", "tensor_reduce", [sq], [ssn], out=ssn[:, :], in_=sq[:, 0:1024].rearrange("p (h d) -> p h d", h=8), axis=AX.X, op=ALU.add)
        E(P, "dve", "tensor_reduce", [sq], [ssr], out=ssr[:, :], in_=sq[:, 1024:1536].rearrange("p (h d) -> p h d", h=8), axis=AX.X, op=ALU.add)
        E(P, "dve", "tensor_tensor", [ssn, ssr], [ssn], out=ssn[:, :], in0=ssn[:, :], in1=ssr[:, :], op=ALU.add)
        rstd_op(P, C, rh, ssn, 192)
        for nb in range(2):
            E(P, "dve", "tensor_tensor", [psw[nb], rh], [tmpn], out=tmpn[:, 4 * nb:4 * nb + 4, :],
              in0=psw[nb][:, :].rearrange("p (h d) -> p h d", h=4), in1=bc(rh[:, 4 * nb:4 * nb + 4].unsqueeze(2), [128, 4, 128]), op=ALU.mult)
        E(P, "dve", "tensor_tensor", [psw[2], rh], [tmpr], out=tmpr[:, :, :], in0=psw[2][:, :].rearrange("p (h d) -> p h d", h=8),
          in1=bc(rh[:, :].unsqueeze(2), [128, 8, 64]), op=ALU.mult)
        E(P, "dve", "tensor_tensor", [tmpn, gq], [nope_bf], out=nope_bf[:, :, :], in0=tmpn[:, :, :],
          in1=bc(gq[:, 0:128].unsqueeze(1), [128, 8, 128]), op=ALU.mult)
        E(P, "pool", "tensor_tensor", [tmpr, gq], [qr], out=qr[:, :, :], in0=tmpr[:, :, :],
          in1=bc(gq[:, 128:192].unsqueeze(1), [128, 8, 64]), op=ALU.mult)
        r = rope(qr, rope_bf, tt, qr, 8)
        E(P, "pool", "tensor_tensor", [rt[0], rt[1]], [rope_bf], out=rope_bf[:, :, 0:32], in0=r[0], in1=r[1], op=ALU.subtract)
        E(P, "pool", "tensor_tensor", [rt[2], rt[3]], [rope_bf], out=rope_bf[:, :, 32:64], in0=r[2], in1=r[3], op=ALU.add)
        ropev = rope_bf[:, :, :].rearrange("p (a b) d -> p a (b d)", b=2)
        for g in range(3):
            pt = ptq[g]
            ptt = ptq_t[g]
            for j in range(4):
                src = nope_bf[:, g * 4 + j, :] if g < 2 else ropev[:, j, :]
                E(P, "pe", "transpose", [nope_bf if g < 2 else rope_bf, C.ident], [ptt], out=pt[:, j * 128:(j + 1) * 128], in_=src, identity=C.ident[:, :])
            eng = ("act", "dve", "act")[g]
            if eng == "act":
                E(P, "act", "activation", [ptt], [qst[b]], out=qst[b][:, g * 4:(g + 1) * 4, :].rearrange("p a b -> p (a b)"), in_=pt[:, :], func=AF.Copy)
            else:
                E(P, "dve", "tensor_copy", [ptt], [qst[b]], out=qst[b][:, g * 4:(g + 1) * 4, :].rearrange("p a b -> p (a b)"), in_=pt[:, :])
        P.dma("sp", qTn_d[:, :, ts], qst[b][:, 0:8, :], [qst[b]], [io["qTn"]], qst[b])
        P.dma("sp", qTr_d[:, :, ts], qst[b][:, 8:12, :], [qst[b]], [io["qTr"]], qst[b])
        sumsq(P, C, ps_b[:, 0:256], ps_b, junkf, ssl)
        rstd_op(P, C, rsl, ssl, 256)
        E(P, "dve", "tensor_scalar", [ps_b, rsl], [qln], out=qln[:, 0:256], in0=ps_b[:, 0:256], scalar1=rsl[:, 0:1], scalar2=None, op0=ALU.mult)
        E(P, "act", "activation", [ps_b], [kr], out=kr[:, :], in_=ps_b[:, 256:320], func=AF.Copy)
        for j in range(2):
            E(P, "pe", "transpose", [qln, C.ident], [pst_t], out=pst[:, j * 128:(j + 1) * 128], in_=qln[:, j * 128:(j + 1) * 128], identity=C.ident[:, :])
        E(P, "act", "activation", [pst_t], [qlT], out=qlT[:, 0:2, :].rearrange("p a b -> p (a b)"), in_=pst[:, 0:256], func=AF.Copy)
        for nb in range(4):
            for k in range(2):
                E(P, "pe", "matmul", [qlT, wkv], [psw[nb]], out=psw[nb][:, :], lhsT=qlT[:, k, :], rhs=wkv[:, k, nb * 512:(nb + 1) * 512],
                  start=(k == 0), stop=(k == 1))
        for nb in range(4):
            kv3 = psw[nb][:, :].rearrange("p (h d) -> p h d", h=2)
            E(P, "act", "activation", [psw[nb]], [sq], out=sq[:, nb * 256:(nb + 1) * 256].rearrange("p (h d) -> p h d", h=2),
              in_=kv3[:, :, 0:128], func=AF.Square)
            E(P, "act", "activation", [psw[nb]], [vst[b]], out=vst[b][:, 2 * nb:2 * nb + 2, :], in_=kv3[:, :, 128:256], func=AF.Copy)
        P.dma("sp", v_d[ts, :], vst[b][:, :, :].rearrange("p h d -> p (h d)"), [vst[b]], [io["kvsend"]], vst[b])
        E(P, "dve", "tensor_reduce", [sq], [ssn], out=ssn[:, :], in_=sq[:, 0:1024].rearrange("p (h d) -> p h d", h=8), axis=AX.X, op=ALU.add)
        sumsq(P, C, kr[:, :], kr, junkf, sskr)
        E(P, "dve", "tensor_scalar", [ssn, sskr], [ssn], out=ssn[:, :], in0=ssn[:, :], scalar1=sskr[:, 0:1], scalar2=None, op0=ALU.add)
        rstd_op(P, C, rh, ssn, 192)
        for nb in range(4):
            kv3 = psw[nb][:, :].rearrange("p (h d) -> p h d", h=2)
            E(P, "dve", "tensor_tensor", [psw[nb], rh], [tmpn], out=tmpn[:, 2 * nb:2 * nb + 2, :], in0=kv3[:, :, 0:128],
              in1=bc(rh[:, 2 * nb:2 * nb + 2].unsqueeze(2), [128, 2, 128]), op=ALU.mult)
        E(P, "dve", "tensor_tensor", [tmpn, gk], [nope_bf], out=nope_bf[:, :, :], in0=tmpn[:, :, :],
          in1=bc(gk[:, 0:128].unsqueeze(1), [128, 8, 128]), op=ALU.mult)
        E(P, "pool", "tensor_tensor", [kr, gk], [krg], out=krg[:, :], in0=kr[:, :], in1=gk[:, 128:192], op=ALU.mult)
        c1 = cos[:, tt, :]
        s1 = sin[:, tt, :]
        E(P, "pool", "tensor_tensor", [krg, cos], [kt4[0]], out=kt4[0][:, :], in0=krg[:, 0:32], in1=c1, op=ALU.mult)
        E(P, "pool", "tensor_tensor", [krg, sin], [kt4[1]], out=kt4[1][:, :], in0=krg[:, 32:64], in1=s1, op=ALU.mult)
        E(P, "pool", "tensor_tensor", [krg, cos], [kt4[2]], out=kt4[2][:, :], in0=krg[:, 32:64], in1=c1, op=ALU.mult)
        E(P, "pool", "tensor_tensor", [krg, sin], [kt4[3]], out=kt4[3][:, :], in0=krg[:, 0:32], in1=s1, op=ALU.mult)
        E(P, "pool", "tensor_tensor", [kt4[0], kt4[1]], [kro], out=kro[:, 0:32], in0=kt4[0][:, :], in1=kt4[1][:, :], op=ALU.subtract)
        E(P, "pool", "tensor_tensor", [kt4[2], kt4[3]], [kro], out=kro[:, 32:64], in0=kt4[2][:, :], in1=kt4[3][:, :], op=ALU.add)
        E(P, "dve", "tensor_tensor", [kro, rh], [rope_bf], out=rope_bf[:, :, :], in0=bc(kro[:, :].unsqueeze(1), [128, 8, 64]),
          in1=bc(rh[:, :].unsqueeze(2), [128, 8, 64]), op=ALU.mult)
        for g in range(3):
            pt = ptq[g]
            ptt = ptq_t[g]
            for j in range(4):
                src = nope_bf[:, g * 4 + j, :] if g < 2 else ropev[:, j, :]
                E(P, "pe", "transpose", [nope_bf if g < 2 else rope_bf, C.ident], [ptt], out=pt[:, j * 128:(j + 1) * 128], in_=src, identity=C.ident[:, :])
            eng = ("dve", "act", "dve")[g]
            if eng == "act":
                E(P, "act", "activation", [ptt], [kst[b]], out=kst[b][:, g * 4:(g + 1) * 4, :].rearrange("p a b -> p (a b)"), in_=pt[:, :], func=AF.Copy)
            else:
                E(P, "dve", "tensor_copy", [ptt], [kst[b]], out=kst[b][:, g * 4:(g + 1) * 4, :].rearrange("p a b -> p (a b)"), in_=pt[:, :])
        P.dma("sp", kTn_d[:, :, ts], kst[b][:, 0:8, :], [kst[b]], [io["kvsend"]], kst[b])
        P.dma("sp", kTr_d[:, :, ts], kst[b][:, 8:12, :], [kst[b]], [io["kvsend"]], kst[b])
    P.barrier()
    if stop_after < 3:
        return
    P.release(m2)

    wch = [P.sb("wch%d" % i, [128, 16, 128], BF16) for i in range(4)]
    sgt = [P.sb("sgt%d" % i, [128, 512]) for i in range(2)]
    ust = [P.sb("ust%d" % i, [128, 512], BF16) for i in range(3)]
    psr = [P.ps("s3ps%d" % i, i) for i in range(8)]
    uT_d = io["uT"].ap
    ut_d = io["utail"].ap.rearrange("p (c q t) -> p c q t", c=8, q=4)
    wi_k = wi.ap.rearrange("(k p) n -> p k n", p=128)
    nw = 0
    npb = 0
    nst = 0

    def fm(wt, qt, pt):
        for k in range(16):
            E(P, "pe", "matmul", [hT, wt], [pt], out=pt[:, :], lhsT=wt[:, k, :], rhs=hT[:, k, qt * 512:(qt + 1) * 512],
              start=(k == 0), stop=(k == 15))

    for c in range(8):
        wv = wch[nw % 4]; nw += 1
        wg = wch[nw % 4]; nw += 1
        WL.load(wv, wv[:, :, :], wi, wi_k[:, :, C_UV + c * 128:C_UV + (c + 1) * 128], [128, 16, 128], engs=("pool",))
        WL.load(wg, wg[:, :, :], wi, wi_k[:, :, C_UG + c * 128:C_UG + (c + 1) * 128], [128, 16, 128], engs=("pool",))
        for qt in range(NQT):
            pv = psr[npb % 8]; npb += 1
            pg = psr[npb % 8]; npb += 1
            fm(wv, qt, pv)
            fm(wg, qt, pg)
            sg = sgt[nst % 2]
            us = ust[nst % 3]
            nst += 1
            E(P, "act", "activation", [pg, pp], [sg], out=sg[:, :], in_=pg[:, :], func=AF.Sigmoid, bias=pp[:, GB + 8 + c:GB + 9 + c])
            E(P, "dve", "scalar_tensor_tensor", [pv, pp, sg], [us], out=us[:, :], in0=pv[:, :], scalar=pp[:, GB + c:GB + c + 1],
              in1=sg[:, :], op0=ALU.add, op1=ALU.mult)
            P.dma("sp", uT_d[c, :, qt, :], us[:, :], [us], [io["uT"]], us)
    for gi in range(16):
        c = gi % 8
        col = (C_MG if gi < 8 else C_CG) + c * 128
        dst = io["sgm"] if gi < 8 else io["sgc"]
        wt = wch[nw % 4]; nw += 1
        WL.load(wt, wt[:, :, :], wi, wi_k[:, :, col:col + 128], [128, 16, 128], engs=("pool",))
        for qt in range(NQT):
            pt = psr[npb % 8]; npb += 1
            fm(wt, qt, pt)
            us = ust[nst % 3]
            nst += 1
            E(P, "act", "activation", [pt], [us], out=us[:, :], in_=pt[:, :], func=AF.Silu)
            P.dma("sp", dst.ap[c, :, qt * 512:(qt + 1) * 512], us[:, :], [us], [dst], us)
    P.barrier()
    for c in range(8):
        P.dma("sp", ut_d[:, c, :, :], uT_d[c, :, :, 480:512], [io["uT"]], [io["utail"]], sgt[0])
    P.barrier()
    P.release(m0)


def core_tiles(core):
    return TILES_A if core % 2 == 0 else TILES_B


def local_tokens(core):
    return np.concatenate([np.arange(g * 512, (g + 1) * 512) for g in core_tiles(core)])


INVF = (1.0 / (10000.0 ** (np.arange(0, 64, 2, dtype=np.float32) / 64.0))).astype(np.float32)

PH1A_IN = [("x", [NT, D], F32), ("mod", [6144], F32), ("norm_g", [D], F32), ("glu_b", [2048], F32),
           ("q_lat_g", [512], F32), ("kv_lat_g", [256], F32), ("pos", [NT], I32), ("invf", [32], F32),
           ("q_norm_g", [192], F32), ("k_norm_g", [192], F32), ("w_in", [D, 4928], F32),
           ("w_q_up", [512, 1536], F32), ("w_kv_up", [256, 2048], F32)]
PH1A_OUT = [("kvsend", [KVROWS, 2048], BF16), ("qTn", [8, 128, NT], BF16), ("qTr", [4, 128, NT], BF16),
            ("uT", [8, 128, 4, 512], BF16), ("utail", [128, 1024], BF16), ("sgm", [8, 128, NT], BF16),
            ("sgc", [8, 128, NT], BF16)]


def build_ph1a(stop_after=99):
    nc = bass.Bass("TRN2", target_bir_lowering=False)
    P = Prog(nc)
    io = {}
    for n, sh, dt in PH1A_IN:
        io[n] = P.dram(n, sh, dt, kind="ExternalInput")
    for n, sh, dt in PH1A_OUT:
        io[n] = P.dram(n, sh, dt, kind="ExternalOutput")
    C = emit_consts(P)
    emit_ph1a(P, C, io, stop_after)
    P.barrier()
    P.finalize()
    return nc


def gtile_src(g):
    if g in TILES_A:
        return 0, TILES_A.index(g)
    return 1, TILES_B.index(g)


def emit_ph2(P, C, io):
    m0 = P.mark()
    kvall = io["kvall"]
    masks = P.sb("masks", [128, 32, 512], BF16)
    kposi = P.sb("kposi", [128, 32], I32)
    kpos = P.sb("kpos", [128, 32])
    mm = P.mark()
    qpb = P.sb("qpb", [128, NT])
    P.dma("sp", qpb[:, :], io["qpos"].ap.partition_broadcast(128), [io["qpos"]], [qpb], qpb)
    E(P, "pool", "iota", [], [kposi], out=kposi[:, :], pattern=[[128, 32]], base=0, channel_multiplier=1)
    E(P, "dve", "tensor_copy", [kposi], [kpos], out=kpos[:, :], in_=kposi[:, :])
    for j in range(NQT):
        for i in range(8):
            kt = 8 * j + i
            E(P, "dve", "tensor_scalar", [qpb, kpos], [masks], out=masks[:, j * 8 + i, :], in0=qpb[:, j * 512:(j + 1) * 512],
              scalar1=kpos[:, kt:kt + 1], scalar2=None, op0=ALU.is_ge)
    P.barrier()
    P.release(mm)
    hb = []
    for i in range(2):
        hb.append(dict(kn=P.sb("kn%d" % i, [128, 4096], BF16), kr=P.sb("krr%d" % i, [128, 4096], BF16),
                       v=P.sb("vv%d" % i, [128, 32, 128], BF16), qn=P.sb("qn%d" % i, [128, NT], BF16),
                       qr=P.sb("qrr%d" % i, [128, NT], BF16), sg=P.sb("sg%d" % i, [128, NT], BF16)))
    pT = [P.sb("pT%d" % i, [128, 512], BF16) for i in range(4)]
    rd = P.sb("rd", [128, 512])
    otmp = P.sb("otmp", [128, 512])
    ost = [P.sb("ost%d" % i, [128, 512], BF16) for i in range(2)]
    psS = [P.ps("psS%d" % i, i) for i in range(4)]
    psO = [P.ps("psO%d" % i, 4 + 2 * i) for i in range(2)]
    psD = [P.ps("psD%d" % i, 5 + 2 * i) for i in range(2)]

    def load_head(h, B):
        r0 = (h % 2) * 64
        for g in range(8):
            rk, loc = gtile_src(g)
            base = rk * KVROWS
            P.dma("sp", B["kn"][:, g * 512:(g + 1) * 512], kvall.ap[base + h * 128: base + (h + 1) * 128, loc * 512:(loc + 1) * 512],
                  [kvall], [B["kn"]], B["kn"])
            rr = base + 1024 + (h // 2) * 128 + r0
            P.dma("sp", B["kr"][r0:r0 + 64, g * 512:(g + 1) * 512], kvall.ap[rr: rr + 64, loc * 512:(loc + 1) * 512],
                  [kvall], [B["kr"]], B["kr"])
            vv = kvall.ap[base + 1536: base + 2560, :].rearrange("r (a c) -> (r a) c", a=2)
            P.dma("sp", B["v"][:, g * 4:(g + 1) * 4, :],
                  vv[loc * 512:(loc + 1) * 512, h * 128:(h + 1) * 128].rearrange("(t p) d -> p t d", p=128),
                  [kvall], [B["v"]], B["v"])
        P.dma("sp", B["qn"][:, :], io["qTn"].ap[h, :, :], [io["qTn"]], [B["qn"]], B["qn"])
        P.dma("sp", B["qr"][r0:r0 + 64, :], io["qTr"].ap[h // 2, r0:r0 + 64, :], [io["qTr"]], [B["qr"]], B["qr"])
        P.dma("sp", B["sg"][:, :], io["sgm"].ap[h, :, :], [io["sgm"]], [B["sg"]], B["sg"])

    load_head(0, hb[0])
    nS = 0
    nO = 0
    for h in range(8):
        B = hb[h % 2]
        if h + 1 < 8:
            load_head(h + 1, hb[(h + 1) % 2])
        r0 = (h % 2) * 64
        for j in range(NQT):
            nk = 8 * (j + 1)
            qs = slice(j * 512, (j + 1) * 512)
            po = psO[nO % 2]
            pd = psD[nO % 2]
            ob = ost[nO % 2]
            nO += 1

            def qk(i, B=B, qs=qs, r0=r0):
                ps = psS[(nS + i) % 4]
                E(P, "pe", "matmul", [B["kn"], B["qn"]], [ps], out=ps[:, :], lhsT=B["kn"][:, i * 128:(i + 1) * 128], rhs=B["qn"][:, qs],
                  start=True, stop=False)
                E(P, "pe", "matmul", [B["kr"], B["qr"]], [ps], out=ps[:, :], lhsT=B["kr"][r0:r0 + 64, i * 128:(i + 1) * 128],
                  rhs=B["qr"][r0:r0 + 64, qs], start=False, stop=True)

            qk(0)
            qk(1)
            for i in range(nk):
                ps = psS[(nS + i) % 4]
                pt = pT[(nS + i) % 4]
                E(P, "act", "activation", [ps], [pt], out=pt[:, :], in_=ps[:, :], func=AF.Exp)
                if i >= 8 * j:
                    E(P, "pool", "tensor_tensor", [pt, masks], [pt], out=pt[:, :], in0=pt[:, :], in1=masks[:, j * 8 + i - 8 * j, :], op=ALU.mult)
                if i + 2 < nk:
                    qk(i + 2)
                E(P, "pe", "matmul", [B["v"], pt], [po], out=po[:, :], lhsT=B["v"][:, i, :], rhs=pt[:, :], start=(i == 0), stop=(i == nk - 1))
                E(P, "pe", "matmul", [C.ones, pt], [pd], out=pd[:, :], lhsT=C.ones[:, :], rhs=pt[:, :], start=(i == 0), stop=(i == nk - 1))
            nS += nk
            E(P, "dve", "reciprocal", [pd], [rd], out=rd[:, :], in_=pd[:, :])
            E(P, "dve", "tensor_tensor", [po, rd], [otmp], out=otmp[:, :], in0=po[:, :], in1=rd[:, :], op=ALU.mult)
            E(P, "dve", "tensor_tensor", [otmp, B["sg"]], [ob], out=ob[:, :], in0=otmp[:, :], in1=B["sg"][:, qs], op=ALU.mult)
            P.dma("sp", io["mlaT"].ap[h, :, qs], ob[:, :], [ob], [io["mlaT"]], ob)
    P.barrier()
    P.release(m0)


PH2_IN = [("kvall", [2 * KVROWS, 2048], BF16), ("qTn", [8, 128, NT], BF16), ("qTr", [4, 128, NT], BF16),
          ("sgm", [8, 128, NT], BF16), ("qpos", [NT], F32)]
PH2_OUT = [("mlaT", [8, 128, NT], BF16)]


def emit_ph3(P, C, io):
    m0 = P.mark()
    gate = P.sb("gate_bc", [128, D])
    P.dma("sp", gate[:, :], io["mod"].ap[4096:6144].partition_broadcast(128), [io["mod"]], [gate], gate)
    wo = P.sb("wo", [128, 16, D], BF16)
    WL = WLoader(P, nstage=4, words=2048)
    wov = io["w_out"].ap.rearrange("(k p) n -> p k n", p=128)
    for k in range(16):
        WL.load(wo, wo[:, k:k + 1, :], io["w_out"], wov[:, k:k + 1, :], [128, 1, D], colscale=(gate, gate[:, :]))
    mix = [P.sb("mix%d" % i, [128, 16, 512], BF16) for i in range(2)]
    xt = [P.sb("x3t%d" % i, [128, D]) for i in range(2)]
    xo = [P.sb("x3o%d" % i, [128, D]) for i in range(2)]
    psr = [P.ps("p3s%d" % i, i) for i in range(8)]
    npb = 0
    for j in range(NQT):
        M = mix[j % 2]
        qs = slice(j * 512, (j + 1) * 512)
        P.dma("sp", M[:, 0:8, :], io["mlaT"].ap[:, :, qs].rearrange("c p t -> p c t"), [io["mlaT"]], [M], M)
        P.dma("sp", M[:, 8:16, :], io["convT"].ap[:, :, qs].rearrange("c p t -> p c t"), [io["convT"]], [M], M)
        for t in range(4):
            tt = j * 4 + t
            X = xt[tt % 2]
            O = xo[tt % 2]
            P.dma("sp", X[:, :], io["x"].ap[tt * 128:(tt + 1) * 128, :], [io["x"]], [X], X)
            for nb in range(4):
                ps = psr[npb % 8]
                npb += 1
                for k in range(16):
                    E(P, "pe", "matmul", [M, wo], [ps], out=ps[:, :], lhsT=M[:, k, t * 128:(t + 1) * 128], rhs=wo[:, k, nb * 512:(nb + 1) * 512],
                      start=(k == 0), stop=(k == 15))
                E(P, "dve", "tensor_tensor", [ps, X], [O], out=O[:, nb * 512:(nb + 1) * 512], in0=ps[:, :], in1=X[:, nb * 512:(nb + 1) * 512], op=ALU.add)
            P.dma("sp", io["xnew"].ap[tt * 128:(tt + 1) * 128, :], O[:, :], [O], [io["xnew"]], O)
    P.barrier()
    P.release(m0)


PH3_IN = [("mlaT", [8, 128, NT], BF16), ("convT", [8, 128, NT], BF16), ("x", [NT, D], F32), ("mod", [6144], F32),
          ("w_out", [D, D], F32)]
PH3_OUT = [("xnew", [NT, D], F32)]


def build_phase(emit, ins, outs):
    nc = bass.Bass("TRN2", target_bir_lowering=False)
    P = Prog(nc)
    io = {}
    for n, sh, dt in ins:
        io[n] = P.dram(n, sh, dt, kind="ExternalInput")
    for n, sh, dt in outs:
        io[n] = P.dram(n, sh, dt, kind="ExternalOutput")
    C = emit_consts(P)
    emit(P, C, io)
    P.barrier()
    P.finalize()
    return nc


def emit_ph1b(P, C, io, stop_after=99):
    m0 = P.mark()
    pp = load_pp(P, C, [(io["dw_b"], io["dw_b"].ap.rearrange("(c p) -> c p", p=128)),
                        (io["conv_ln_g"], io["conv_ln_g"].ap.rearrange("(c p) -> c p", p=128)),
                        (io["conv_ln_b"], io["conv_ln_b"].ap.rearrange("(c p) -> c p", p=128)),
                        (io["b_pw"], io["b_pw"].ap.rearrange("(c p) -> c p", p=128))], "pp1b")
    DWB, LNG, LNB, BPW = 0, 8, 16, 24
    wT = P.sb("wT", [128, 8, 32])
    dwr = P.sb("dwr", [128, 1024])
    P.dma("sp", dwr[0:31, :], io["dw_w"].ap, [io["dw_w"]], [dwr], dwr)
    ptw = P.ps("ptw", 6, 0, 512)
    for c in range(8):
        E(P, "pe", "transpose", [dwr, C.identf], [ptw], out=ptw[:, c * 32:c * 32 + 31], in_=dwr[0:31, c * 128:(c + 1) * 128],
          identity=C.identf[0:31, 0:31])
    E(P, "dve", "tensor_copy", [ptw], [wT], out=wT[:, :, 0:31], in_=ptw[:, 0:256].rearrange("p (c j) -> p c j", c=8)[:, :, 0:31])
    diag = P.sb("diag", [128, 8 * 31, 128], BF16)
    n = 0
    for c in range(8):
        for j in range(31):
            if n % 2 == 0:
                E(P, "dve", "tensor_scalar", [C.ident, wT], [diag], out=diag[:, c * 31 + j, :], in0=C.ident[:, :], scalar1=wT[:, c, j:j + 1],
                  scalar2=None, op0=ALU.mult)
            else:
                E(P, "act", "activation", [C.ident, wT], [diag], out=diag[:, c * 31 + j, :], in_=C.ident[:, :], func=AF.Copy, scale=wT[:, c, j:j + 1])
            n += 1
    wpw = P.sb("wpw", [128, 8, 1024], BF16)
    WL = WLoader(P, nstage=2, words=2048)
    wpv = io["w_pw"].ap.rearrange("(k p) n -> p k n", p=128)
    for k in range(0, 8, 2):
        WL.load(wpw, wpw[:, k:k + 2, :], io["w_pw"], wpv[:, k:k + 2, :], [128, 2, 1024], engs=("pool",))
    tl = P.sb("tl", [128, 2, 8, 4, 32], BF16)
    P.dma("sp", tl[:, :, :, :, :].rearrange("p r c q t -> p r (c q t)"), io["tails"].ap.rearrange("(r p) n -> p r n", p=128),
          [io["tails"]], [tl], tl)
    tlf = P.sb("tlf", [128, 2, 8, 4, 32])
    E(P, "dve", "tensor_copy", [tl], [tlf], out=tlf[:, :, :, :, :].rearrange("p r c q t -> p (r c q t)"),
      in_=tl[:, :, :, :, :].rearrange("p r c q t -> p (r c q t)"))
    selb = P.sb("selb", [128, 32])
    P.dma("sp", selb[:, :], io["sel"].ap.partition_broadcast(128), [io["sel"]], [selb], selb)
    hacc = P.sb("hacc", [128, 8, 32])
    htmp = P.sb("htmp", [128, 8, 32])
    ub = [P.sb("ub%d" % i, [128, 8, 544], BF16) for i in range(2)]
    ustg = [P.sb("ustg%d" % i, [128, 8, 512], BF16) for i in range(1)]
    vT = P.sb("vT", [128, 8, 512])
    vb = [P.sb("vb%d" % i, [128, 512], BF16) for i in range(2)]
    sqb = [P.sb("sqb%d" % i, [128, 512], BF16) for i in range(2)]
    mean = P.sb("mean", [128, 512]); msq = P.sb("msq", [128, 512]); rstd = P.sb("rstd", [128, 512])
    t1 = [P.sb("t1_%d" % i, [128, 512]) for i in range(2)]
    act = P.sb("act", [128, 8, 512], BF16)
    sgb = [P.sb("sgb%d" % i, [128, 8, 512], BF16) for i in range(2)]
    cst = [P.sb("cst%d" % i, [128, 512], BF16) for i in range(2)]
    psc = [P.ps("psc%d" % i, i) for i in range(4)]
    pS = P.ps("pS", 4)
    pQ = P.ps("pQ", 5)
    pso = [P.ps("pso%d" % i, 6 + i) for i in range(2)]
    nv = 0
    no = 0
    if stop_after < 1:
        P.barrier()
        return
    for j in range(NQT):
        U = ub[j % 2]
        SG = sgb[j % 2]
        qs = slice(j * 512, (j + 1) * 512)
        US = ustg[0]
        P.dma("sp", US[:, :, :], io["uT"].ap[:, :, j, :].rearrange("c p t -> p c t"), [io["uT"]], [US], US)
        P.dma("sp", SG[:, :, :], io["sgc"].ap[:, :, qs].rearrange("c p t -> p c t"), [io["sgc"]], [SG], SG)
        if stop_after == 1:
            P.barrier()
            return
        for s in range(8):
            src = tlf[:, s // 4, :, s % 4, :]
            sb_ = bc(selb[:, j * 8 + s:j * 8 + s + 1].unsqueeze(2), [128, 8, 32])
            if s == 0:
                E(P, "dve", "tensor_tensor", [tlf, selb], [hacc], out=hacc[:, :, :], in0=src, in1=sb_, op=ALU.mult)
                if stop_after == 11:
                    P.barrier()
                    return
            else:
                E(P, "dve", "tensor_tensor", [tlf, selb], [htmp], out=htmp[:, :, :], in0=src, in1=sb_, op=ALU.mult)
                E(P, "dve", "tensor_tensor", [hacc, htmp], [hacc], out=hacc[:, :, :], in0=hacc[:, :, :], in1=htmp[:, :, :], op=ALU.add)
        if stop_after == 12:
            P.barrier()
            return
        HT = ub[1] if stop_after == 13 else U
        HO = 512 if stop_after == 14 else 0
        E(P, "dve", "tensor_copy", [hacc], [HT], out=HT[:, :, HO:HO + 32], in_=hacc[:, :, :])
        E(P, "pool", "tensor_copy", [US], [U], out=U[:, :, 32:544], in_=US[:, :, :])
        if stop_after < 2 or stop_after in (11, 12, 13):
            P.barrier()
            return
        for c in range(8):
            pc = psc[nv % 4]
            VB = vb[nv % 2]
            SQ = sqb[nv % 2]
            nv += 1
            for jj in range(31):
                E(P, "pe", "matmul", [diag, U], [pc], out=pc[:, :], lhsT=diag[:, c * 31 + jj, :], rhs=U[:, c, 2 + jj:2 + jj + 512],
                  start=(jj == 0), stop=(jj == 30))
            if stop_after == 20:
                P.barrier(); return
            E(P, "act", "activation", [pc, pp], [vT], out=vT[:, c, :], in_=pc[:, :], func=AF.Identity, bias=pp[:, DWB + c:DWB + c + 1])
            if stop_after == 21:
                P.barrier(); return
            E(P, "act", "activation", [vT], [SQ], out=SQ[:, :], in_=vT[:, c, :], func=AF.Square)
            if stop_after == 22:
                P.barrier(); return
            E(P, "dve", "tensor_copy", [vT], [VB], out=VB[:, :], in_=vT[:, c, :])
            if stop_after == 23:
                P.barrier(); return
            E(P, "pe", "matmul", [C.ones, VB], [pS], out=pS[:, :], lhsT=C.ones[:, :], rhs=VB[:, :], start=(c == 0), stop=(c == 7))
            E(P, "pe", "matmul", [C.ones, SQ], [pQ], out=pQ[:, :], lhsT=C.ones[:, :], rhs=SQ[:, :], start=(c == 0), stop=(c == 7))
        if stop_after < 3:
            P.barrier()
            return
        E(P, "dve", "tensor_scalar", [pS], [mean], out=mean[:, :], in0=pS[:, :], scalar1=1.0 / 1024, scalar2=None, op0=ALU.mult)
        E(P, "dve", "tensor_tensor", [mean], [msq], out=msq[:, :], in0=mean[:, :], in1=mean[:, :], op=ALU.mult)
        E(P, "dve", "scalar_tensor_tensor", [pQ, msq], [msq], out=msq[:, :], in0=pQ[:, :], scalar=1.0 / 1024, in1=msq[:, :],
          op0=ALU.mult, op1=ALU.subtract)
        E(P, "act", "activation", [msq, C.eps], [rstd], out=rstd[:, :], in_=msq[:, :], func=AF.Sqrt, bias=C.eps[:, 0:1])
        E(P, "dve", "reciprocal", [rstd], [rstd], out=rstd[:, :], in_=rstd[:, :])
        for c in range(8):
            T = t1[c % 2]
            E(P, "dve", "tensor_tensor", [vT, mean], [T], out=T[:, :], in0=vT[:, c, :], in1=mean[:, :], op=ALU.subtract)
            E(P, "dve", "tensor_tensor", [T, rstd], [T], out=T[:, :], in0=T[:, :], in1=rstd[:, :], op=ALU.mult)
            E(P, "act", "activation", [T, pp], [act], out=act[:, c, :], in_=T[:, :], func=AF.Silu, scale=pp[:, LNG + c:LNG + c + 1],
              bias=pp[:, LNB + c:LNB + c + 1])
        if stop_after < 4:
            P.barrier()
            return
        for co in range(8):
            po = pso[no % 2]
            CS = cst[no % 2]
            no += 1
            for ci in range(8):
                E(P, "pe", "matmul", [wpw, act], [po], out=po[:, :], lhsT=wpw[:, ci, co * 128:(co + 1) * 128], rhs=act[:, ci, :],
                  start=(ci == 0), stop=(ci == 7))
            E(P, "dve", "scalar_tensor_tensor", [po, pp, SG], [CS], out=CS[:, :], in0=po[:, :], scalar=pp[:, BPW + co:BPW + co + 1],
              in1=SG[:, co, :], op0=ALU.add, op1=ALU.mult)
            P.dma("sp", io["convT"].ap[co, :, qs], CS[:, :], [CS], [io["convT"]], CS)
    P.barrier()
    P.release(m0)


PH1B_IN = [("uT", [8, 128, 4, 512], BF16), ("tails", [256, 1024], BF16), ("sel", [32], F32), ("dw_w", [31, 1024], F32),
           ("dw_b", [1024], F32), ("conv_ln_g", [1024], F32), ("conv_ln_b", [1024], F32), ("w_pw", [1024, 1024], F32),
           ("b_pw", [1024], F32), ("sgc", [8, 128, NT], BF16)]
PH1B_OUT = [("convT", [8, 128, NT], BF16)]


def halo_sel(core):
    sel = np.zeros(32, np.float32)
    for j, g in enumerate(core_tiles(core)):
        if g > 0:
            rk, loc = gtile_src(g - 1)
            sel[j * 8 + rk * 4 + loc] = 1.0
    return sel


def emit_ph0(P, C, io):
    m0 = P.mark()
    ct = P.sb("ct", [128, 64])
    P.dma("sp", ct[:, :], io["cT"].ap, [io["cT"]], [ct], ct)
    E(P, "act", "activation", [ct], [ct], out=ct[:, :], in_=ct[:, :], func=AF.Silu)
    bias = P.sb("adab", [128, 2 * 768])
    P.dma("sp", bias[0:4, :], io["ada_bs"].ap.rearrange("l n -> (l n)").partition_broadcast(4), [io["ada_bs"]], [bias], bias)
    wst = [P.sb("adaw%d" % i, [128, 4, 384]) for i in range(3)]
    res = P.sb("adares", [128, 2 * 768])
    pss = [P.ps("ph0ps%d" % i, i) for i in range(4)]
    ctv = ct[:, :].rearrange("p (k b) -> p k b", b=4)
    n = 0
    for l in range(2):
        wv = io["ada_ws"].ap[l].rearrange("(k p) n -> p k n", p=128)
        for hf in range(2):
            ps = pss[l * 2 + hf]
            for kg in range(4):
                W = wst[n % 3]
                n += 1
                P.dma("sp", W[:, :, :], wv[:, kg * 4:(kg + 1) * 4, hf * 384:(hf + 1) * 384], [io["ada_ws"]], [W], W)
                for kk in range(4):
                    k = kg * 4 + kk
                    E(P, "pe", "matmul", [ct, W], [ps], out=ps[0:4, 0:384], lhsT=ctv[:, k, :], rhs=W[:, kk, :], start=(k == 0), stop=(k == 15))
            o = (l * 2 + hf) * 384
            E(P, "dve", "tensor_tensor", [ps, bias], [res], out=res[0:4, o:o + 384], in0=ps[0:4, 0:384], in1=bias[0:4, o:o + 384], op=ALU.add)
    P.dma("sp", io["mods"].ap.rearrange("b l n -> b (l n)"), res[0:4, :], [res], [io["mods"]], res)
    P.barrier()
    P.release(m0)


PH0_IN = [("cT", [128, 64], F32), ("ada_ws", [2, D, 768], F32), ("ada_bs", [2, 768], F32)]
PH0_OUT = [("mods", [4, 2, 768], F32)]

PHB_IN = [("uT", [8, 128, 4, 512], BF16), ("tails", [256, 1024], BF16), ("sel", [32], F32), ("dw_w", [31, 1024], F32),
          ("dw_b", [1024], F32), ("conv_ln_g", [1024], F32), ("conv_ln_b", [1024], F32), ("w_pw", [1024, 1024], F32),
          ("b_pw", [1024], F32), ("sgc", [8, 128, NT], BF16), ("kvall", [2 * KVROWS, 2048], BF16), ("qTn", [8, 128, NT], BF16),
          ("qTr", [4, 128, NT], BF16), ("sgm", [8, 128, NT], BF16), ("qpos", [NT], F32), ("x", [NT, D], F32),
          ("mod", [6144], F32), ("w_out", [D, D], F32)]
PHB_OUT = [("xnew", [NT, D], F32)]


def emit_back(P, C, io):
    emit_ph1b(P, C, io)
    emit_ph2(P, C, io)
    emit_ph3(P, C, io)


def build_back():
    nc = bass.Bass("TRN2", target_bir_lowering=False)
    P = Prog(nc)
    io = {}
    for n, sh, dt in PHB_IN:
        io[n] = P.dram(n, sh, dt, kind="ExternalInput")
    for n, sh, dt in PHB_OUT:
        io[n] = P.dram(n, sh, dt, kind="ExternalOutput")
    io["convT"] = P.dram("convT", [8, 128, NT], BF16)
    io["mlaT"] = P.dram("mlaT", [8, 128, NT], BF16)
    C = emit_consts(P)
    emit_back(P, C, io)
    P.barrier()
    P.finalize()
    return nc


def _c(a, dt=None):
    a = np.ascontiguousarray(a)
    return a if dt is None else a.astype(dt)


def kernel_unfused(x, c, positions, ada_w, ada_b, norm_g, w_in, q_lat_g, w_q_up, kv_lat_g, w_kv_up, q_norm_g, k_norm_g,
                   glu_b, dw_w, dw_b, conv_ln_g, conv_ln_b, w_pw, b_pw, w_out):
    cores = list(range(8))
    toks = [local_tokens(cr) for cr in cores]
    cT = _c(np.asarray(c, np.float32).reshape(4, 16, 128).transpose(2, 1, 0).reshape(128, 64))
    ins = [dict(cT=cT, ada_ws=_c(ada_w[:, :, i * 768:(i + 1) * 768]), ada_bs=_c(ada_b[:, i * 768:(i + 1) * 768])) for i in cores]
    r0 = run_bass_kernel_spmd(build_phase(emit_ph0, PH0_IN, PH0_OUT), ins, core_ids=cores)
    mods = np.concatenate([np.asarray(r0.results[i]["mods"]) for i in cores], axis=2)
    xl = [_c(x[cr // 2][toks[cr]]) for cr in cores]
    nc_a = build_ph1a()
    nc_b = build_back()
    for l in range(2):
        ins = []
        for cr in cores:
            b = cr // 2
            ins.append(dict(x=xl[cr], mod=_c(mods[b, l]), norm_g=_c(norm_g[l]), glu_b=_c(glu_b[l]), q_lat_g=_c(q_lat_g[l]),
                            kv_lat_g=_c(kv_lat_g[l]), pos=_c(positions[b][toks[cr]], np.int32), invf=INVF, q_norm_g=_c(q_norm_g[l]),
                            k_norm_g=_c(k_norm_g[l]), w_in=_c(w_in[l]), w_q_up=_c(w_q_up[l]), w_kv_up=_c(w_kv_up[l])))
        ra = run_bass_kernel_spmd(nc_a, ins, core_ids=cores).results
        ins = []
        for cr in cores:
            b = cr // 2
            kvall = np.concatenate([np.asarray(ra[2 * b]["kvsend"]), np.asarray(ra[2 * b + 1]["kvsend"])], axis=0)
            tails = np.concatenate([np.asarray(ra[2 * b]["utail"]), np.asarray(ra[2 * b + 1]["utail"])], axis=0)
            ins.append(dict(uT=np.asarray(ra[cr]["uT"]), tails=tails, sel=halo_sel(cr), dw_w=_c(dw_w[l]), dw_b=_c(dw_b[l]),
                            conv_ln_g=_c(conv_ln_g[l]), conv_ln_b=_c(conv_ln_b[l]), w_pw=_c(w_pw[l]), b_pw=_c(b_pw[l]),
                            sgc=np.asarray(ra[cr]["sgc"]), kvall=kvall, qTn=np.asarray(ra[cr]["qTn"]), qTr=np.asarray(ra[cr]["qTr"]),
                            sgm=np.asarray(ra[cr]["sgm"]), qpos=toks[cr].astype(np.float32), x=xl[cr], mod=_c(mods[b, l]),
                            w_out=_c(w_out[l])))
        rb = run_bass_kernel_spmd(nc_b, ins, core_ids=cores).results
        xl = [np.asarray(rb[cr]["xnew"]) for cr in cores]
    out = np.empty((4, 4096, D), np.float32)
    for cr in cores:
        out[cr // 2][toks[cr]] = xl[cr]
    return out


def kernel(**inputs):
    inputs = {k: np.asarray(v) for k, v in inputs.items()}
    return kernel_unfused(**inputs)
```

```python
import numpy as np
import ml_dtypes
import concourse.bass as bass
import concourse.mybir as mybir
from concourse.bass_utils import run_bass_kernel_spmd

F32 = mybir.dt.float32
BF16 = mybir.dt.bfloat16
I32 = mybir.dt.int32
ALU = mybir.AluOpType
AF = mybir.ActivationFunctionType
AX = mybir.AxisListType

ENGS = ("pe", "act", "dve", "pool", "sp")


class Tile:
    __slots__ = ("name", "ap", "last_w", "reads", "sem")

    def __init__(self, name, ap):
        self.name = name
        self.ap = ap
        self.last_w = None
        self.reads = []
        self.sem = None

    def __getitem__(self, k):
        return self.ap[k]


class Grp:
    __slots__ = ("sem", "final", "closed")


class Op:
    __slots__ = ("eng", "fn", "deps", "signal", "val", "grp", "inc", "seq")


class Prog:
    def __init__(self, nc, arena_f32=51712, n_dma_sems=64):
        self.nc = nc
        self.ops = {e: [] for e in ENGS}
        self.nsig = {e: 0 for e in ENGS}
        self.esem = {e: nc.alloc_semaphore("sem_" + e) for e in ENGS}
        self.dsems = [nc.alloc_semaphore("dsem%d" % i) for i in range(n_dma_sems)]
        self.dsem_cum = [0] * n_dma_sems
        self.dsem_grp = [None] * n_dma_sems
        self.dsem_free = list(range(n_dma_sems))
        self.arena = nc.alloc_sbuf_tensor("arena", [128, arena_f32], F32)
        self.arena_n = arena_f32
        self.top = 0
        self.psum = nc.alloc_psum_tensor("psum_all", [128, 8 * 512], F32)
        self.seq = 0
        self.live_tiles = []
        self.ndram = 0

    def sb(self, name, shape, dtype=F32):
        esz = 4 if dtype in (F32, I32) else 2
        free = int(np.prod(shape[1:]))
        nf32 = (free * esz + 3) // 4
        nf32 = (nf32 + 7) // 8 * 8
        assert self.top + nf32 <= self.arena_n, "SBUF arena overflow at %s (%d + %d)" % (name, self.top, nf32)
        ap = self.arena.ap()[0:shape[0], self.top:self.top + nf32]
        self.top += nf32
        if dtype != F32:
            ap = ap.bitcast(dtype)
        ap = ap[:, 0:free]
        if len(shape) > 2:
            names = " ".join("d%d" % i for i in range(len(shape) - 1))
            kw = {"d%d" % i: shape[i + 1] for i in range(len(shape) - 1)}
            ap = ap.rearrange("p (%s) -> p %s" % (names, names), **kw)
        t = Tile(name, ap)
        self.live_tiles.append(t)
        return t

    def mark(self):
        return (self.top, len(self.live_tiles))

    def release(self, mark):
        top, n = mark
        for t in self.live_tiles[n:]:
            if t.sem is not None:
                self.dsem_free.append(t.sem)
                t.sem = None
        del self.live_tiles[n:]
        self.top = top

    def ps(self, name, bank, off=0, n=512, dtype=F32, parts=128):
        ap = self.psum.ap()[0:parts, bank * 512 + off: bank * 512 + off + n]
        if dtype != F32:
            ap = ap.bitcast(dtype)
        return Tile(name, ap)

    def psv(self, name, bank0, nwords, parts=128):
        return Tile(name, self.psum.ap()[0:parts, bank0 * 512: bank0 * 512 + nwords])

    def dram(self, name, shape, dtype, kind="Internal"):
        t = self.nc.dram_tensor(name, list(shape), dtype, kind=kind)
        return Tile(name, t.ap())

    def _mk(self, eng, fn, reads, writes):
        op = Op()
        op.eng = eng
        op.fn = fn
        op.deps = []
        op.signal = False
        op.val = None
        op.grp = None
        op.inc = None
        self.seq += 1
        op.seq = self.seq
        deps = op.deps
        for t in reads:
            if t.last_w is not None:
                deps.append(t.last_w)
        for t in writes:
            if t.last_w is not None:
                deps.append(t.last_w)
            deps.extend(t.reads)
        for d in deps:
            if d.grp is not None:
                d.grp.closed = True
        return op

    def _commit(self, op, reads, writes):
        for t in writes:
            t.last_w = op
            t.reads = []
        for t in reads:
            t.reads.append(op)
        self.ops[op.eng].append(op)

    def op(self, eng, fn, reads=(), writes=()):
        op = self._mk(eng, fn, reads, writes)
        self._commit(op, reads, writes)
        return op

    def _sem_for(self, tile):
        if tile.sem is None:
            assert self.dsem_free, "out of DMA semaphores"
            tile.sem = self.dsem_free.pop(0)
        return tile.sem

    def dma(self, q, out, in_, reads, writes, semtile, inc=16, fn=None):
        si = self._sem_for(semtile)
        if fn is None:
            def fn(e, out=out, in_=in_):
                return e.dma_start(out=out, in_=in_)
        op = self._mk(q, fn, reads, writes)
        g = self.dsem_grp[si]
        if g is None or g.closed:
            if g is not None:
                op.deps.append(("grp", g))
            g = Grp()
            g.sem = si
            g.closed = False
            g.final = self.dsem_cum[si]
            self.dsem_grp[si] = g
        self.dsem_cum[si] += inc
        g.final = self.dsem_cum[si]
        op.grp = g
        op.inc = inc
        self._commit(op, reads, writes)
        return op

    def allgather(self, groups, src, dst, semtile, src_ap=None, dst_ap=None):
        sa = src.ap if src_ap is None else src_ap
        da = dst.ap if dst_ap is None else dst_ap

        def fn(e, sa=sa, da=da, groups=groups):
            return e.collective_compute("AllGather", ALU.bypass, replica_groups=groups, ins=[sa], outs=[da])
        return self.dma("pool", None, None, [src], [dst], semtile, inc=1, fn=fn)

    def barrier(self):
        lasts = []
        for e in ENGS:
            if self.ops[e]:
                lasts.append(self.ops[e][-1])
        grps = [("grp", g) for g in self.dsem_grp if g is not None]
        for _, g in grps:
            g.closed = True
        for e in ENGS:
            op = Op()
            op.eng = e
            op.fn = None
            op.deps = list(lasts) + list(grps)
            op.signal = False
            op.val = None
            op.grp = None
            op.inc = None
            self.seq += 1
            op.seq = self.seq
            self.ops[e].append(op)
        for t in self.live_tiles:
            t.last_w = None
            t.reads = []

    def finalize(self):
        nc = self.nc
        for e in ENGS:
            for op in self.ops[e]:
                for d in op.deps:
                    if isinstance(d, tuple):
                        d[1].closed = True
                    elif d.grp is not None:
                        d.grp.closed = True
                    else:
                        d.signal = True
        for e in ENGS:
            cnt = 0
            for op in self.ops[e]:
                if op.grp is None and op.signal and op.fn is not None:
                    cnt += 1
                    op.val = cnt
        ops = self.ops
        esem = self.esem
        dsems = self.dsems

        def replay(eng_name, e):
            seen = {}
            for op in ops[eng_name]:
                waits = {}
                for d in op.deps:
                    if isinstance(d, tuple):
                        g = d[1]
                        key = ("d", g.sem)
                        v = g.final
                    elif d.grp is not None:
                        key = ("d", d.grp.sem)
                        v = d.grp.final
                    else:
                        if d.fn is None:
                            continue
                        if d.eng == eng_name and eng_name in ("pe", "sp"):
                            continue
                        key = ("e", d.eng)
                        v = d.val
                    if waits.get(key, 0) < v:
                        waits[key] = v
                for key, v in waits.items():
                    if seen.get(key, 0) >= v:
                        continue
                    seen[key] = v
                    sem = dsems[key[1]] if key[0] == "d" else esem[key[1]]
                    e.wait_ge(sem, v)
                if op.fn is None:
                    continue
                ins = op.fn(e)
                if op.grp is not None:
                    ins.then_inc(dsems[op.grp.sem], op.inc)
                elif op.signal:
                    ins.then_inc(esem[eng_name], 1)

        with nc.Block() as block:
            block.tensor(lambda e: replay("pe", e))
            block.scalar(lambda e: replay("act", e))
            block.vector(lambda e: replay("dve", e))
            block.gpsimd(lambda e: replay("pool", e))
            block.sync(lambda e: replay("sp", e))


def bc(ap, shape):
    return ap.broadcast_to(list(shape))


def E(P, eng, method, reads, writes, **kw):
    return P.op(eng, lambda e: getattr(e, method)(**kw), reads, writes)


D = 2048
NT = 2048
NTT = 16
NQT = 4
NO_UTAIL = 0
EPS = 1e-6
C_QL, C_KV, C_KR, C_MG, C_UV, C_UG, C_CG = 0, 512, 768, 832, 1856, 2880, 3904
TILES_A = (0, 3, 4, 7)
TILES_B = (1, 2, 5, 6)
KVROWS = 2560


class Consts:
    pass


def emit_consts(P):
    C = Consts()
    C.identf = P.sb("identf", [128, 128])
    C.ident = P.sb("ident", [128, 128], BF16)
    C.ones = P.sb("ones", [128, 128], BF16)
    C.eps = P.sb("eps", [128, 1])
    E(P, "pool", "memset", [], [C.identf], ap=C.identf[:, :], constant=0.0)
    E(P, "pool", "affine_select", [C.identf], [C.identf], out=C.identf[:, :], in_=C.identf[:, :],
      pattern=[[-1, 128]], compare_op=ALU.not_equal, fill=1.0, base=0, channel_multiplier=1)
    E(P, "dve", "tensor_copy", [C.identf], [C.ident], out=C.ident[:, :], in_=C.identf[:, :])
    E(P, "pool", "memset", [], [C.ones], ap=C.ones[:, :], constant=1.0)
    E(P, "pool", "memset", [], [C.eps], ap=C.eps[:, :], constant=EPS)
    return C


def rstd_op(P, C, out, ss, n):
    E(P, "act", "activation", [ss, C.eps], [out], out=out[:, :], in_=ss[:, :], func=AF.Sqrt,
      scale=1.0 / n, bias=C.eps[:, 0:1])
    E(P, "dve", "reciprocal", [out], [out], out=out[:, :], in_=out[:, :])


class WLoader:
    def __init__(self, P, nstage=4, words=2048, q="sp"):
        self.P = P
        self.stg = [P.sb("wstg%d" % i, [128, words]) for i in range(nstage)]
        self.words = words
        self.i = 0
        self.ci = 0
        self.q = q
        self.engs = ("pool", "dve", "act")

    def load(self, dst, dst_view, src_tile, src_ap, shape, rowscale=None, colscale=None, engs=None):
        P = self.P
        a, n = shape[1], shape[2]
        assert a * n <= self.words
        st = self.stg[self.i % len(self.stg)]
        self.i += 1
        sv = st[:, 0:a * n].rearrange("p (a n) -> p a n", a=a)
        P.dma(self.q, sv, src_ap, [src_tile], [st], st)
        engs = engs or self.engs
        if rowscale is not None:
            rt, rap = rowscale
            for j in range(a):
                eng = ("dve", "act")[self.ci % 2]
                self.ci += 1
                if eng == "dve":
                    E(P, "dve", "tensor_scalar", [st, rt], [dst], out=dst_view[:, j, :], in0=sv[:, j, :],
                      scalar1=rap[:, j:j + 1], scalar2=None, op0=ALU.mult)
                else:
                    E(P, "act", "activation", [st, rt], [dst], out=dst_view[:, j, :], in_=sv[:, j, :],
                      func=AF.Copy, scale=rap[:, j:j + 1])
        elif colscale is not None:
            ct, cap = colscale
            eng = ("dve", "pool")[self.ci % 2]
            self.ci += 1
            E(P, eng, "tensor_tensor", [st, ct], [dst], out=dst_view, in0=sv,
              in1=bc(cap.unsqueeze(1), [128, a, n]), op=ALU.mult)
        else:
            eng = engs[self.ci % len(engs)]
            self.ci += 1
            if eng == "act":
                E(P, "act", "activation", [st], [dst], out=dst_view, in_=sv, func=AF.Copy)
            else:
                E(P, eng, "tensor_copy", [st], [dst], out=dst_view, in_=sv)


def load_pp(P, C, pieces, name, pbank=7):
    rows = sum(ap.shape[0] for _, ap in pieces)
    assert rows <= 128
    out = P.sb(name, [128, rows])
    stk = P.sb(name + "_stk", [128, 128])
    r = 0
    for dt, ap in pieces:
        n = ap.shape[0]
        P.dma("sp", stk[r:r + n, :], ap, [dt], [stk], stk)
        r += n
    pt = P.ps(name + "_pt", pbank, 0, 128)
    E(P, "pe", "transpose", [stk, C.identf], [pt], out=pt[:, 0:rows], in_=stk[0:rows, :], identity=C.identf[0:rows, 0:rows])
    E(P, "dve", "tensor_copy", [pt], [out], out=out[:, :], in_=pt[:, 0:rows])
    return out


def sumsq(P, C, ps_ap, ps_tile, junk, ss):
    E(P, "pool", "memset", [], [ss], ap=ss[:, :], constant=0.0)
    E(P, "act", "activation", [ps_tile, ss], [junk, ss], out=junk[:, 0:ps_ap.shape[1]], in_=ps_ap, func=AF.Square,
      accum_out=ss[:, 0:1])


def emit_ph1a(P, C, io, stop_after=99):
    m0 = P.mark()
    x, wi = io["x"], io["w_in"]
    pp = load_pp(P, C, [(io["mod"], io["mod"].ap.rearrange("(c p) -> c p", p=128)),
                        (io["norm_g"], io["norm_g"].ap.rearrange("(c p) -> c p", p=128)),
                        (io["glu_b"], io["glu_b"].ap.rearrange("(c p) -> c p", p=128)),
                        (io["q_lat_g"], io["q_lat_g"].ap.rearrange("(c p) -> c p", p=128)),
                        (io["kv_lat_g"], io["kv_lat_g"].ap.rearrange("(c p) -> c p", p=128))], "pp")
    SH, SC, NG, GB, QLG, KLG = 0, 16, 48, 64, 80, 84
    gm = P.sb("gm", [128, 16])
    cos = P.sb("cos", [128, 16, 32])
    sin = P.sb("sin", [128, 16, 32])
    gq = P.sb("gq", [128, 192])
    gk = P.sb("gk", [128, 192])
    hT = P.sb("hT", [128, 16, NT], BF16)
    WL = WLoader(P, nstage=4, words=2048)
    w1 = P.sb("w1", [128, 16, 832], BF16)
    wq = P.sb("wq", [128, 4, 1536], BF16)
    wkv = P.sb("wkv", [128, 2, 2048], BF16)
    ms0 = P.mark()
    E(P, "dve", "scalar_tensor_tensor", [pp], [gm], out=gm[:, :], in0=pp[:, SC:SC + 16], scalar=1.0,
      in1=pp[:, NG:NG + 16], op0=ALU.add, op1=ALU.mult)
    posi = P.sb("posi", [128, 128], I32)
    posf = P.sb("posf", [128, 128])
    P.dma("sp", posi[0:16, :], io["pos"].ap.rearrange("(c p) -> c p", p=128), [io["pos"]], [posi], posi)
    E(P, "dve", "tensor_copy", [posi], [posf], out=posf[0:16, :], in_=posi[0:16, :])
    ptp = P.ps("pos_pt", 6, 0, 128)
    E(P, "pe", "transpose", [posf, C.identf], [ptp], out=ptp[:, 0:16], in_=posf[0:16, :], identity=C.identf[0:16, 0:16])
    posT = P.sb("posT", [128, 16])
    E(P, "dve", "tensor_copy", [ptp], [posT], out=posT[:, :], in_=ptp[:, 0:16])
    invf = P.sb("invf", [128, 32])
    P.dma("sp", invf[:, :], io["invf"].ap.partition_broadcast(128), [io["invf"]], [invf], invf)
    ang = P.sb("ang", [128, 16, 32])
    angi = P.sb("angi", [128, 16, 32], I32)
    angf = P.sb("angf", [128, 16, 32])
    angm = P.sb("angm", [128, 16, 32])
    E(P, "dve", "tensor_tensor", [posT, invf], [ang], out=ang[:, :, :], in0=bc(posT[:, :].unsqueeze(2), [128, 16, 32]),
      in1=bc(invf[:, :].unsqueeze(1), [128, 16, 32]), op=ALU.mult)
    for dst, off in ((sin, 0.0), (cos, 0.25)):
        E(P, "dve", "tensor_scalar", [ang], [angm], out=angm[:, :, :], in0=ang[:, :, :], scalar1=1.0 / (2 * np.pi),
          scalar2=off, op0=ALU.mult, op1=ALU.add)
        E(P, "dve", "tensor_copy", [angm], [angi], out=angi[:, :, :], in_=angm[:, :, :])
        E(P, "dve", "tensor_copy", [angi], [angf], out=angf[:, :, :], in_=angi[:, :, :])
        E(P, "dve", "tensor_tensor", [angm, angf], [angm], out=angm[:, :, :], in0=angm[:, :, :], in1=angf[:, :, :], op=ALU.subtract)
        E(P, "dve", "tensor_single_scalar", [angm], [angf], out=angf[:, :, :], in_=angm[:, :, :], scalar=0.5, op=ALU.is_gt)
        E(P, "dve", "tensor_tensor", [angm, angf], [angm], out=angm[:, :, :], in0=angm[:, :, :], in1=angf[:, :, :], op=ALU.subtract)
        E(P, "act", "activation", [angm], [dst], out=dst[:, :, :], in_=angm[:, :, :], func=AF.Sin, scale=float(2 * np.pi))
    P.dma("sp", gq[:, :], io["q_norm_g"].ap.partition_broadcast(128), [io["q_norm_g"]], [gq], gq)
    P.dma("sp", gk[:, :], io["k_norm_g"].ap.partition_broadcast(128), [io["k_norm_g"]], [gk], gk)
    E(P, "dve", "tensor_scalar", [gq], [gq], out=gq[:, :], in0=gq[:, :], scalar1=float(192.0 ** -0.5), scalar2=None, op0=ALU.mult)

    wiv = wi.ap.rearrange("(k p) n -> p k n", p=128)
    for k in range(0, 16, 2):
        WL.load(w1, w1[:, k:k + 2, :], wi, wiv[:, k:k + 2, 0:832], [128, 2, 832])
    wqv = io["w_q_up"].ap.rearrange("(k p) (h j) -> p k h j", p=128, j=192)
    for k in range(4):
        WL.load(wq, wq[:, k:k + 1, 0:1024].rearrange("p a (h j) -> p (a h) j", j=128), io["w_q_up"],
                wqv[:, k, :, 0:128], [128, 8, 128], rowscale=None)
        WL.load(wq, wq[:, k:k + 1, 1024:1536].rearrange("p a (h j) -> p (a h) j", j=64), io["w_q_up"],
                wqv[:, k, :, 128:192], [128, 8, 64], rowscale=None)
    wkvv = io["w_kv_up"].ap.rearrange("(k p) n -> p k n", p=128)
    for k in range(2):
        WL.load(wkv, wkv[:, k:k + 1, :], io["w_kv_up"], wkvv[:, k:k + 1, :], [128, 1, 2048])
    for k in range(4):
        E(P, "dve", "tensor_scalar", [wq, pp], [wq], out=wq[:, k, :], in0=wq[:, k, :], scalar1=pp[:, QLG + k:QLG + k + 1],
          scalar2=None, op0=ALU.mult)
    for k in range(2):
        E(P, "dve", "tensor_scalar", [wkv, pp], [wkv], out=wkv[:, k, :], in0=wkv[:, k, :], scalar1=pp[:, KLG + k:KLG + k + 1],
          scalar2=None, op0=ALU.mult)

    P.barrier()
    if stop_after < 1:
        return
    P.release(ms0)
    m1 = P.mark()
    xt = [P.sb("xt%d" % i, [128, D]) for i in range(2)]
    xn = [P.sb("xn%d" % i, [128, D], BF16) for i in range(2)]
    junk = P.sb("junk", [128, D], BF16)
    ssx = [P.sb("ssx%d" % i, [128, 1]) for i in range(2)]
    rsx = [P.sb("rsx%d" % i, [128, 1]) for i in range(2)]
    ptr = [P.ps("s1pt%d" % i, i, 0, 256, BF16) for i in range(4)]
    for tt in range(NTT):
        b = tt % 2
        P.dma("sp", xt[b][:, :], x.ap[tt * 128:(tt + 1) * 128, :], [x], [xt[b]], xt[b])
        E(P, "pool", "memset", [], [ssx[b]], ap=ssx[b][:, :], constant=0.0)
        E(P, "act", "activation", [xt[b], ssx[b]], [junk, ssx[b]], out=junk[:, :], in_=xt[b][:, :], func=AF.Square,
          accum_out=ssx[b][:, 0:1])
        rstd_op(P, C, rsx[b], ssx[b], D)
        E(P, "dve", "tensor_scalar", [xt[b], rsx[b]], [xn[b]], out=xn[b][:, :], in0=xt[b][:, :], scalar1=rsx[b][:, 0:1],
          scalar2=None, op0=ALU.mult)
        for g in range(4):
            pt = ptr[g]
            for j in range(4):
                c = g * 4 + j
                E(P, "pe", "transpose", [xn[b], C.ident], [pt], out=pt[:, j * 128:(j + 1) * 128],
                  in_=xn[b][:, c * 128:(c + 1) * 128], identity=C.ident[:, :])
            for j in range(4):
                c = g * 4 + j
                if j % 2 == 0:
                    E(P, "act", "activation", [pt, gm, pp], [hT], out=hT[:, c, tt * 128:(tt + 1) * 128],
                      in_=pt[:, j * 128:(j + 1) * 128], func=AF.Identity, scale=gm[:, c:c + 1], bias=pp[:, SH + c:SH + c + 1])
                else:
                    E(P, "dve", "tensor_scalar", [pt, gm, pp], [hT], out=hT[:, c, tt * 128:(tt + 1) * 128],
                      in0=pt[:, j * 128:(j + 1) * 128], scalar1=gm[:, c:c + 1], scalar2=pp[:, SH + c:SH + c + 1],
                      op0=ALU.mult, op1=ALU.add)
    P.barrier()
    if stop_after < 2:
        return
    P.release(m1)

    m2 = P.mark()
    ps_a = P.ps("ps_a", 0)
    ps_b = P.ps("ps_b", 1)
    pt2 = P.ps("pt2", 2, 0, 512, BF16)
    pt7 = P.ps("pt7", 7, 0, 512, BF16)
    psw = [P.ps("psw%d" % i, 3 + i) for i in range(4)]
    pst = Tile("pst", pt2.ap[:, 0:512]); pst_t = pt2
    ptq = [Tile("ptq0", pt7.ap[:, 0:512]), Tile("ptq1", pt7.ap[:, 512:1024]), Tile("ptq2", pt2.ap[:, 512:1024])]
    ptq_t = [pt7, pt7, pt2]
    junkf = P.sb("junkf", [128, 512])
    ssl = P.sb("ssl", [128, 1]); rsl = P.sb("rsl", [128, 1])
    qln = P.sb("qln", [128, 512], BF16)
    qlT = P.sb("qlT", [128, 4, 128], BF16)
    sq = P.sb("sq", [128, 1536])
    ssn = P.sb("ssn", [128, 8]); ssr = P.sb("ssr", [128, 8]); rh = P.sb("rh", [128, 8])
    tmpn = P.sb("tmpn", [128, 8, 128])
    tmpr = P.sb("tmpr", [128, 8, 64])
    qr = P.sb("qr", [128, 8, 64])
    rt = [P.sb("rt%d" % i, [128, 8, 32]) for i in range(4)]
    nope_bf = P.sb("nope_bf", [128, 8, 128], BF16)
    rope_bf = P.sb("rope_bf", [128, 8, 64], BF16)
    qst = [P.sb("qst%d" % i, [128, 12, 128], BF16) for i in range(2)]
    kst = [P.sb("kst%d" % i, [128, 12, 128], BF16) for i in range(2)]
    vst = [P.sb("vst%d" % i, [128, 8, 128], BF16) for i in range(2)]
    kr = P.sb("kr", [128, 64]); krg = P.sb("krg", [128, 64]); kro = P.sb("kro", [128, 64])
    kt4 = [P.sb("kt4_%d" % i, [128, 32]) for i in range(4)]
    sskr = P.sb("sskr", [128, 1])
    kTn_d = io["kvsend"].ap[0:1024, :].rearrange("(h d) t -> d h t", d=128)
    kTr_d = io["kvsend"].ap[1024:1536, :].rearrange("(h d) t -> d h t", d=128)
    v_d = io["kvsend"].ap[1536:2560, :].rearrange("r (a c) -> (r a) c", a=2)
    qTn_d = io["qTn"].ap.rearrange("h d t -> d h t")
    qTr_d = io["qTr"].ap.rearrange("h d t -> d h t")

    def rope(src, dst_bf, tt, src_tile, nh):
        cs = bc(cos[:, tt:tt + 1, :], [128, nh, 32])
        sn = bc(sin[:, tt:tt + 1, :], [128, nh, 32])
        x1 = src[:, :, 0:32]
        x2 = src[:, :, 32:64]
        r = [t[:, 0:nh, :] for t in rt]
        E(P, "pool", "tensor_tensor", [src_tile, cos], [rt[0]], out=r[0], in0=x1, in1=cs, op=ALU.mult)
        E(P, "pool", "tensor_tensor", [src_tile, sin], [rt[1]], out=r[1], in0=x2, in1=sn, op=ALU.mult)
        E(P, "pool", "tensor_tensor", [src_tile, cos], [rt[2]], out=r[2], in0=x2, in1=cs, op=ALU.mult)
        E(P, "pool", "tensor_tensor", [src_tile, sin], [rt[3]], out=r[3], in0=x1, in1=sn, op=ALU.mult)
        return r

    for tt in range(NTT):
        ts = slice(tt * 128, (tt + 1) * 128)
        b = tt % 2
        for k in range(16):
            E(P, "pe", "matmul", [hT, w1], [ps_a], out=ps_a[:, :], lhsT=hT[:, k, ts], rhs=w1[:, k, 0:512], start=(k == 0), stop=(k == 15))
        for k in range(16):
            E(P, "pe", "matmul", [hT, w1], [ps_b], out=ps_b[:, 0:320], lhsT=hT[:, k, ts], rhs=w1[:, k, 512:832], start=(k == 0), stop=(k == 15))
        sumsq(P, C, ps_a[:, :], ps_a, junkf, ssl)
        rstd_op(P, C, rsl, ssl, 512)
        E(P, "dve", "tensor_scalar", [ps_a, rsl], [qln], out=qln[:, :], in0=ps_a[:, :], scalar1=rsl[:, 0:1], scalar2=None, op0=ALU.mult)
        for j in range(4):
            E(P, "pe", "transpose", [qln, C.ident], [pst_t], out=pst[:, j * 128:(j + 1) * 128], in_=qln[:, j * 128:(j + 1) * 128], identity=C.ident[:, :])
        E(P, "act", "activation", [pst_t], [qlT], out=qlT[:, :, :].rearrange("p a b -> p (a b)"), in_=pst[:, :], func=AF.Copy)
        for nb in range(3):
            for k in range(4):
                E(P, "pe", "matmul", [qlT, wq], [psw[nb]], out=psw[nb][:, :], lhsT=qlT[:, k, :], rhs=wq[:, k, nb * 512:(nb + 1) * 512],
                  start=(k == 0), stop=(k == 3))
        for nb in range(3):
            E(P, "act", "activation", [psw[nb]], [sq], out=sq[:, nb * 512:(nb + 1) * 512], in_=psw[nb][:, :], func=AF.Square)
        E(P, "dve", "tensor_reduce", [sq], [ssn], out=ssn[:, :], in_=sq[:, 0:1024].rearrange("p (h d) -> p h d", h=8), axis=AX.X, op=ALU.add)
        E(P, "dve", "tensor_reduce", [sq], [ssr], out=ssr[:, :], in_=sq[:, 1024:1536].rearrange("p (h d) -> p h d", h=8), axis=AX.X, op=ALU.add)
        E(P, "dve", "tensor_tensor", [ssn, ssr], [ssn], out=ssn[:, :], in0=ssn[:, :], in1=ssr[:, :], op=ALU.add)
        rstd_op(P, C, rh, ssn, 192)
        for nb in range(2):
            E(P, "dve", "tensor_tensor", [psw[nb], rh], [tmpn], out=tmpn[:, 4 * nb:4 * nb + 4, :],
              in0=psw[nb][:, :].rearrange("p (h d) -> p h d", h=4), in1=bc(rh[:, 4 * nb:4 * nb + 4].unsqueeze(2), [128, 4, 128]), op=ALU.mult)
        E(P, "dve", "tensor_tensor", [psw[2], rh], [tmpr], out=tmpr[:, :, :], in0=psw[2][:, :].rearrange("p (h d) -> p h d", h=8),
          in1=bc(rh[:, :].unsqueeze(2), [128, 8, 64]), op=ALU.mult)
        E(P, "dve", "tensor_tensor", [tmpn, gq], [nope_bf], out=nope_bf[:, :, :], in0=tmpn[:, :, :],
          in1=bc(gq[:, 0:128].unsqueeze(1), [128, 8, 128]), op=ALU.mult)
        E(P, "pool", "tensor_tensor", [tmpr, gq], [qr], out=qr[:, :, :], in0=tmpr[:, :, :],
          in1=bc(gq[:, 128:192].unsqueeze(1), [128, 8, 64]), op=ALU.mult)
        r = rope(qr, rope_bf, tt, qr, 8)
        E(P, "pool", "tensor_tensor", [rt[0], rt[1]], [rope_bf], out=rope_bf[:, :, 0:32], in0=r[0], in1=r[1], op=ALU.subtract)
        E(P, "pool", "tensor_tensor", [rt[2], rt[3]], [rope_bf], out=rope_bf[:, :, 32:64], in0=r[2], in1=r[3], op=ALU.add)
        ropev = rope_bf[:, :, :].rearrange("p (a b) d -> p a (b d)", b=2)
        for g in range(3):
            pt = ptq[g]
            ptt = ptq_t[g]
            for j in range(4):
                src = nope_bf[:, g * 4 + j, :] if g < 2 else ropev[:, j, :]
                E(P, "pe", "transpose", [nope_bf if g < 2 else rope_bf, C.ident], [ptt], out=pt[:, j * 128:(j + 1) * 128], in_=src, identity=C.ident[:, :])
            eng = ("act", "dve", "act")[g]
            if eng == "act":
                E(P, "act", "activation", [ptt], [qst[b]], out=qst[b][:, g * 4:(g + 1) * 4, :].rearrange("p a b -> p (a b)"), in_=pt[:, :], func=AF.Copy)
            else:
                E(P, "dve", "tensor_copy", [ptt], [qst[b]], out=qst[b][:, g * 4:(g + 1) * 4, :].rearrange("p a b -> p (a b)"), in_=pt[:, :])
        P.dma("sp", qTn_d[:, :, ts], qst[b][:, 0:8, :], [qst[b]], [io["qTn"]], qst[b])
        P.dma("sp", qTr_d[:, :, ts], qst[b][:, 8:12, :], [qst[b]], [io["qTr"]], qst[b])
        sumsq(P, C, ps_b[:, 0:256], ps_b, junkf, ssl)
        rstd_op(P, C, rsl, ssl, 256)
        E(P, "dve", "tensor_scalar", [ps_b, rsl], [qln], out=qln[:, 0:256], in0=ps_b[:, 0:256], scalar1=rsl[:, 0:1], scalar2=None, op0=ALU.mult)
        E(P, "act", "activation", [ps_b], [kr], out=kr[:, :], in_=ps_b[:, 256:320], func=AF.Copy)
        for j in range(2):
            E(P, "pe", "transpose", [qln, C.ident], [pst_t], out=pst[:, j * 128:(j + 1) * 128], in_=qln[:, j * 128:(j + 1) * 128], identity=C.ident[:, :])
        E(P, "act", "activation", [pst_t], [qlT], out=qlT[:, 0:2, :].rearrange("p a b -> p (a b)"), in_=pst[:, 0:256], func=AF.Copy)
        for nb in range(4):
            for k in range(2):
                E(P, "pe", "matmul", [qlT, wkv], [psw[nb]], out=psw[nb][:, :], lhsT=qlT[:, k, :], rhs=wkv[:, k, nb * 512:(nb + 1) * 512],
                  start=(k == 0), stop=(k == 1))
        for nb in range(4):
            kv3 = psw[nb][:, :].rearrange("p (h d) -> p h d", h=2)
            E(P, "act", "activation", [psw[nb]], [sq], out=sq[:, nb * 256:(nb + 1) * 256].rearrange("p (h d) -> p h d", h=2),
              in_=kv3[:, :, 0:128], func=AF.Square)
            E(P, "act", "activation", [psw[nb]], [vst[b]], out=vst[b][:, 2 * nb:2 * nb + 2, :], in_=kv3[:, :, 128:256], func=AF.Copy)
        P.dma("sp", v_d[ts, :], vst[b][:, :, :].rearrange("p h d -> p (h d)"), [vst[b]], [io["kvsend"]], vst[b])
        E(P, "dve", "tensor_reduce", [sq], [ssn], out=ssn[:, :], in_=sq[:, 0:1024].rearrange("p (h d) -> p h d", h=8), axis=AX.X, op=ALU.add)
        sumsq(P, C, kr[:, :], kr, junkf, sskr)
        E(P, "dve", "tensor_scalar", [ssn, sskr], [ssn], out=ssn[:, :], in0=ssn[:, :], scalar1=sskr[:, 0:1], scalar2=None, op0=ALU.add)
        rstd_op(P, C, rh, ssn, 192)
        for nb in range(4):
            kv3 = psw[nb][:, :].rearrange("p (h d) -> p h d", h=2)
            E(P, "dve", "tensor_tensor", [psw[nb], rh], [tmpn], out=tmpn[:, 2 * nb:2 * nb + 2, :], in0=kv3[:, :, 0:128],
              in1=bc(rh[:, 2 * nb:2 * nb + 2].unsqueeze(2), [128, 2, 128]), op=ALU.mult)
        E(P, "dve", "tensor_tensor", [tmpn, gk], [nope_bf], out=nope_bf[:, :, :], in0=tmpn[:, :, :],
          in1=bc(gk[:, 0:128].unsqueeze(1), [128, 8, 128]), op=ALU.mult)
        E(P, "pool", "tensor_tensor", [kr, gk], [krg], out=krg[:, :], in0=kr[:, :], in1=gk[:, 128:192], op=ALU.mult)
        c1 = cos[:, tt, :]
        s1 = sin[:, tt, :]
        E(P, "pool", "tensor_tensor", [krg, cos], [kt4[0]], out=kt4[0][:, :], in0=krg[:, 0:32], in1=c1, op=ALU.mult)
        E(P, "pool", "tensor_tensor", [krg, sin], [kt4[1]], out=kt4[1][:, :], in0=krg[:, 32:64], in1=s1, op=ALU.mult)
        E(P, "pool", "tensor_tensor", [krg, cos], [kt4[2]], out=kt4[2][:, :], in0=krg[:, 32:64], in1=c1, op=ALU.mult)
        E(P, "pool", "tensor_tensor", [krg, sin], [kt4[3]], out=kt4[3][:, :], in0=krg[:, 0:32], in1=s1, op=ALU.mult)
        E(P, "pool", "tensor_tensor", [kt4[0], kt4[1]], [kro], out=kro[:, 0:32], in0=kt4[0][:, :], in1=kt4[1][:, :], op=ALU.subtract)
        E(P, "pool", "tensor_tensor", [kt4[2], kt4[3]], [kro], out=kro[:, 32:64], in0=kt4[2][:, :], in1=kt4[3][:, :], op=ALU.add)
        E(P, "dve", "tensor_tensor", [kro, rh], [rope_bf], out=rope_bf[:, :, :], in0=bc(kro[:, :].unsqueeze(1), [128, 8, 64]),
          in1=bc(rh[:, :].unsqueeze(2), [128, 8, 64]), op=ALU.mult)
        for g in range(3):
            pt = ptq[g]
            ptt = ptq_t[g]
            for j in range(4):
                src = nope_bf[:, g * 4 + j, :] if g < 2 else ropev[:, j, :]
                E(P, "pe", "transpose", [nope_bf if g < 2 else rope_bf, C.ident], [ptt], out=pt[:, j * 128:(j + 1) * 128], in_=src, identity=C.ident[:, :])
            eng = ("dve", "act", "dve")[g]
            if eng == "act":
                E(P, "act", "activation", [ptt], [kst[b]], out=kst[b][:, g * 4:(g + 1) * 4, :].rearrange("p a b -> p (a b)"), in_=pt[:, :], func=AF.Copy)
            else:
                E(P, "dve", "tensor_copy", [ptt], [kst[b]], out=kst[b][:, g * 4:(g + 1) * 4, :].rearrange("p a b -> p (a b)"), in_=pt[:, :])
        P.dma("sp", kTn_d[:, :, ts], kst[b][:, 0:8, :], [kst[b]], [io["kvsend"]], kst[b])
        P.dma("sp", kTr_d[:, :, ts], kst[b][:, 8:12, :], [kst[b]], [io["kvsend"]], kst[b])
    P.barrier()
    if stop_after < 3:
        return
    P.release(m2)

    wch = [P.sb("wch%d" % i, [128, 16, 128], BF16) for i in range(4)]
    sgt = [P.sb("sgt%d" % i, [128, 512]) for i in range(2)]
    ust = [P.sb("ust%d" % i, [128, 512], BF16) for i in range(3)]
    psr = [P.ps("s3ps%d" % i, i) for i in range(8)]
    uT_d = io["uT"].ap
    ut_d = io["utail"].ap.rearrange("p (c q t) -> p c q t", c=8, q=4)
    wi_k = wi.ap.rearrange("(k p) n -> p k n", p=128)
    nw = 0
    npb = 0
    nst = 0

    def fm(wt, qt, pt):
        for k in range(16):
            E(P, "pe", "matmul", [hT, wt], [pt], out=pt[:, :], lhsT=wt[:, k, :], rhs=hT[:, k, qt * 512:(qt + 1) * 512],
              start=(k == 0), stop=(k == 15))

    s3cols = []
    for c in range(8):
        s3cols += [C_UV + c * 128, C_UG + c * 128]
    s3cols += [(C_MG if gi < 8 else C_CG) + (gi % 8) * 128 for gi in range(16)]
    wtile = {}

    def ensure(i):
        if i < len(s3cols) and i not in wtile:
            wt_ = wch[i % 4]
            WL.load(wt_, wt_[:, :, :], wi, wi_k[:, :, s3cols[i]:s3cols[i] + 128], [128, 16, 128], engs=("pool",))
            wtile[i] = wt_

    ensure(0)
    ensure(1)
    for c in range(8):
        ensure(2 * c + 2)
        ensure(2 * c + 3)
        wv = wtile[2 * c]
        wg = wtile[2 * c + 1]
        for qt in range(NQT):
            pv = psr[npb % 8]; npb += 1
            pg = psr[npb % 8]; npb += 1
            fm(wv, qt, pv)
            fm(wg, qt, pg)
            sg = sgt[nst % 2]
            us = ust[nst % 3]
            nst += 1
            E(P, "act", "activation", [pg, pp], [sg], out=sg[:, :], in_=pg[:, :], func=AF.Sigmoid, bias=pp[:, GB + 8 + c:GB + 9 + c])
            E(P, "dve", "scalar_tensor_tensor", [pv, pp, sg], [us], out=us[:, :], in0=pv[:, :], scalar=pp[:, GB + c:GB + c + 1],
              in1=sg[:, :], op0=ALU.add, op1=ALU.mult)
            P.dma("sp", uT_d[c, :, qt, :], us[:, :], [us], [io["uT"]], us)
    for gi in range(16):
        c = gi % 8
        col = (C_MG if gi < 8 else C_CG) + c * 128
        dst = io["sgm"] if gi < 8 else io["sgc"]
        ensure(16 + gi + 1)
        wt = wtile[16 + gi]
        for qt in range(NQT):
            pt = psr[npb % 8]; npb += 1
            fm(wt, qt, pt)
            us = ust[nst % 3]
            nst += 1
            E(P, "act", "activation", [pt], [us], out=us[:, :], in_=pt[:, :], func=AF.Silu)
            P.dma("sp", dst.ap[c, :, qt * 512:(qt + 1) * 512], us[:, :], [us], [dst], us)
    P.barrier()
    for c in range(8):
        P.dma("sp", ut_d[:, c, :, :], uT_d[c, :, :, 480:512], [io["uT"]], [io["utail"]], sgt[0])
    P.barrier()
    P.release(m0)


def core_tiles(core):
    return TILES_A if core % 2 == 0 else TILES_B


def local_tokens(core):
    return np.concatenate([np.arange(g * 512, (g + 1) * 512) for g in core_tiles(core)])


INVF = (1.0 / (10000.0 ** (np.arange(0, 64, 2, dtype=np.float32) / 64.0))).astype(np.float32)

PH1A_IN = [("x", [NT, D], F32), ("mod", [6144], F32), ("norm_g", [D], F32), ("glu_b", [2048], F32),
           ("q_lat_g", [512], F32), ("kv_lat_g", [256], F32), ("pos", [NT], I32), ("invf", [32], F32),
           ("q_norm_g", [192], F32), ("k_norm_g", [192], F32), ("w_in", [D, 4928], F32),
           ("w_q_up", [512, 1536], F32), ("w_kv_up", [256, 2048], F32)]
PH1A_OUT = [("kvsend", [KVROWS, 2048], BF16), ("qTn", [8, 128, NT], BF16), ("qTr", [4, 128, NT], BF16),
            ("uT", [8, 128, 4, 512], BF16), ("utail", [128, 1024], BF16), ("sgm", [8, 128, NT], BF16),
            ("sgc", [8, 128, NT], BF16)]


def build_ph1a(stop_after=99):
    nc = bass.Bass("TRN2", target_bir_lowering=False)
    P = Prog(nc)
    io = {}
    for n, sh, dt in PH1A_IN:
        io[n] = P.dram(n, sh, dt, kind="ExternalInput")
    for n, sh, dt in PH1A_OUT:
        io[n] = P.dram(n, sh, dt, kind="ExternalOutput")
    C = emit_consts(P)
    emit_ph1a(P, C, io, stop_after)
    P.barrier()
    P.finalize()
    return nc


def gtile_src(g):
    if g in TILES_A:
        return 0, TILES_A.index(g)
    return 1, TILES_B.index(g)


def kv_row(rk, r):
    return (r // 256) * 512 + rk * 256 + (r % 256)


def kv_exchange_host(a, b):
    return np.concatenate([np.concatenate([a[i * 256:(i + 1) * 256], b[i * 256:(i + 1) * 256]], 0) for i in range(KVROWS // 256)], 0)


def emit_ph2(P, C, io):
    m0 = P.mark()
    kvall = io["kvall"]
    masks = P.sb("masks", [128, 32, 512], BF16)
    kposi = P.sb("kposi", [128, 32], I32)
    kpos = P.sb("kpos", [128, 32])
    mm = P.mark()
    qpb = P.sb("qpb", [128, NT])
    P.dma("sp", qpb[:, :], io["qpos"].ap.partition_broadcast(128), [io["qpos"]], [qpb], qpb)
    E(P, "pool", "iota", [], [kposi], out=kposi[:, :], pattern=[[128, 32]], base=0, channel_multiplier=1)
    E(P, "dve", "tensor_copy", [kposi], [kpos], out=kpos[:, :], in_=kposi[:, :])
    for j in range(NQT):
        for i in range(8):
            kt = 8 * j + i
            E(P, "dve", "tensor_scalar", [qpb, kpos], [masks], out=masks[:, j * 8 + i, :], in0=qpb[:, j * 512:(j + 1) * 512],
              scalar1=kpos[:, kt:kt + 1], scalar2=None, op0=ALU.is_ge)
    P.barrier()
    P.release(mm)
    hb = []
    for i in range(2):
        hb.append(dict(kn=P.sb("kn%d" % i, [128, 4096], BF16), kr=P.sb("krr%d" % i, [128, 4096], BF16),
                       v=P.sb("vv%d" % i, [128, 32, 128], BF16), qn=P.sb("qn%d" % i, [128, NT], BF16),
                       qr=P.sb("qrr%d" % i, [128, NT], BF16), sg=P.sb("sg%d" % i, [128, NT], BF16)))
    pT = [P.sb("pT%d" % i, [128, 512], BF16) for i in range(4)]
    rd = P.sb("rd", [128, 512])
    otmp = P.sb("otmp", [128, 512])
    ost = [P.sb("ost%d" % i, [128, 512], BF16) for i in range(2)]
    psS = [P.ps("psS%d" % i, i) for i in range(4)]
    psO = [P.ps("psO%d" % i, 4 + 2 * i) for i in range(2)]
    psD = [P.ps("psD%d" % i, 5 + 2 * i) for i in range(2)]

    def load_head(h, B):
        r0 = (h % 2) * 64
        for g in range(8):
            rk, loc = gtile_src(g)
            rn = kv_row(rk, h * 128)
            P.dma("sp", B["kn"][:, g * 512:(g + 1) * 512], kvall.ap[rn: rn + 128, loc * 512:(loc + 1) * 512],
                  [kvall], [B["kn"]], B["kn"])
            rr = kv_row(rk, 1024 + (h // 2) * 128 + r0)
            P.dma("sp", B["kr"][r0:r0 + 64, g * 512:(g + 1) * 512], kvall.ap[rr: rr + 64, loc * 512:(loc + 1) * 512],
                  [kvall], [B["kr"]], B["kr"])
            rv = kv_row(rk, 1536 + loc * 256)
            vv = kvall.ap[rv: rv + 256, :].rearrange("r (a c) -> (r a) c", a=2)
            P.dma("sp", B["v"][:, g * 4:(g + 1) * 4, :],
                  vv[:, h * 128:(h + 1) * 128].rearrange("(t p) d -> p t d", p=128),
                  [kvall], [B["v"]], B["v"])
        P.dma("sp", B["qn"][:, :], io["qTn"].ap[h, :, :], [io["qTn"]], [B["qn"]], B["qn"])
        P.dma("sp", B["qr"][r0:r0 + 64, :], io["qTr"].ap[h // 2, r0:r0 + 64, :], [io["qTr"]], [B["qr"]], B["qr"])
        P.dma("sp", B["sg"][:, :], io["sgm"].ap[h, :, :], [io["sgm"]], [B["sg"]], B["sg"])

    load_head(0, hb[0])
    nS = 0
    nO = 0
    for h in range(8):
        B = hb[h % 2]
        if h + 1 < 8:
            load_head(h + 1, hb[(h + 1) % 2])
        r0 = (h % 2) * 64
        for j in range(NQT):
            nk = 8 * (j + 1)
            qs = slice(j * 512, (j + 1) * 512)
            po = psO[nO % 2]
            pd = psD[nO % 2]
            ob = ost[nO % 2]
            nO += 1

            def qk(i, B=B, qs=qs, r0=r0):
                ps = psS[(nS + i) % 4]
                E(P, "pe", "matmul", [B["kn"], B["qn"]], [ps], out=ps[:, :], lhsT=B["kn"][:, i * 128:(i + 1) * 128], rhs=B["qn"][:, qs],
                  start=True, stop=False)
                E(P, "pe", "matmul", [B["kr"], B["qr"]], [ps], out=ps[:, :], lhsT=B["kr"][r0:r0 + 64, i * 128:(i + 1) * 128],
                  rhs=B["qr"][r0:r0 + 64, qs], start=False, stop=True)

            qk(0)
            qk(1)
            for i in range(nk):
                ps = psS[(nS + i) % 4]
                pt = pT[(nS + i) % 4]
                E(P, "act", "activation", [ps], [pt], out=pt[:, :], in_=ps[:, :], func=AF.Exp)
                if i >= 8 * j:
                    E(P, "pool", "tensor_tensor", [pt, masks], [pt], out=pt[:, :], in0=pt[:, :], in1=masks[:, j * 8 + i - 8 * j, :], op=ALU.mult)
                if i + 2 < nk:
                    qk(i + 2)
                E(P, "pe", "matmul", [B["v"], pt], [po], out=po[:, :], lhsT=B["v"][:, i, :], rhs=pt[:, :], start=(i == 0), stop=(i == nk - 1))
                E(P, "pe", "matmul", [C.ones, pt], [pd], out=pd[:, :], lhsT=C.ones[:, :], rhs=pt[:, :], start=(i == 0), stop=(i == nk - 1))
            nS += nk
            E(P, "dve", "reciprocal", [pd], [rd], out=rd[:, :], in_=pd[:, :])
            E(P, "dve", "tensor_tensor", [po, rd], [otmp], out=otmp[:, :], in0=po[:, :], in1=rd[:, :], op=ALU.mult)
            E(P, "dve", "tensor_tensor", [otmp, B["sg"]], [ob], out=ob[:, :], in0=otmp[:, :], in1=B["sg"][:, qs], op=ALU.mult)
            P.dma("sp", io["mlaT"].ap[h, :, qs], ob[:, :], [ob], [io["mlaT"]], ob)
    P.barrier()
    P.release(m0)


PH2_IN = [("kvall", [2 * KVROWS, 2048], BF16), ("qTn", [8, 128, NT], BF16), ("qTr", [4, 128, NT], BF16),
          ("sgm", [8, 128, NT], BF16), ("qpos", [NT], F32)]
PH2_OUT = [("mlaT", [8, 128, NT], BF16)]


def emit_ph3(P, C, io):
    m0 = P.mark()
    gate = P.sb("gate_bc", [128, D])
    P.dma("sp", gate[:, :], io["mod"].ap[4096:6144].partition_broadcast(128), [io["mod"]], [gate], gate)
    wo = P.sb("wo", [128, 16, D], BF16)
    WL = WLoader(P, nstage=4, words=2048)
    wov = io["w_out"].ap.rearrange("(k p) n -> p k n", p=128)
    for k in range(16):
        WL.load(wo, wo[:, k:k + 1, :], io["w_out"], wov[:, k:k + 1, :], [128, 1, D], colscale=(gate, gate[:, :]))
    mix = [P.sb("mix%d" % i, [128, 16, 512], BF16) for i in range(2)]
    xt = [P.sb("x3t%d" % i, [128, D]) for i in range(2)]
    xo = [P.sb("x3o%d" % i, [128, D]) for i in range(2)]
    psr = [P.ps("p3s%d" % i, i) for i in range(8)]
    npb = 0
    for j in range(NQT):
        M = mix[j % 2]
        qs = slice(j * 512, (j + 1) * 512)
        P.dma("sp", M[:, 0:8, :], io["mlaT"].ap[:, :, qs].rearrange("c p t -> p c t"), [io["mlaT"]], [M], M)
        P.dma("sp", M[:, 8:16, :], io["convT"].ap[:, :, qs].rearrange("c p t -> p c t"), [io["convT"]], [M], M)
        for t in range(4):
            tt = j * 4 + t
            X = xt[tt % 2]
            O = xo[tt % 2]
            P.dma("sp", X[:, :], io["x"].ap[tt * 128:(tt + 1) * 128, :], [io["x"]], [X], X)
            for nb in range(4):
                ps = psr[npb % 8]
                npb += 1
                for k in range(16):
                    E(P, "pe", "matmul", [M, wo], [ps], out=ps[:, :], lhsT=M[:, k, t * 128:(t + 1) * 128], rhs=wo[:, k, nb * 512:(nb + 1) * 512],
                      start=(k == 0), stop=(k == 15))
                E(P, "dve", "tensor_tensor", [ps, X], [O], out=O[:, nb * 512:(nb + 1) * 512], in0=ps[:, :], in1=X[:, nb * 512:(nb + 1) * 512], op=ALU.add)
            P.dma("sp", io["xnew"].ap[tt * 128:(tt + 1) * 128, :], O[:, :], [O], [io["xnew"]], O)
    P.barrier()
    P.release(m0)


PH3_IN = [("mlaT", [8, 128, NT], BF16), ("convT", [8, 128, NT], BF16), ("x", [NT, D], F32), ("mod", [6144], F32),
          ("w_out", [D, D], F32)]
PH3_OUT = [("xnew", [NT, D], F32)]


def build_phase(emit, ins, outs):
    nc = bass.Bass("TRN2", target_bir_lowering=False)
    P = Prog(nc)
    io = {}
    for n, sh, dt in ins:
        io[n] = P.dram(n, sh, dt, kind="ExternalInput")
    for n, sh, dt in outs:
        io[n] = P.dram(n, sh, dt, kind="ExternalOutput")
    C = emit_consts(P)
    emit(P, C, io)
    P.barrier()
    P.finalize()
    return nc


def emit_ph1b(P, C, io, stop_after=99):
    m0 = P.mark()
    pp = load_pp(P, C, [(io["dw_b"], io["dw_b"].ap.rearrange("(c p) -> c p", p=128)),
                        (io["conv_ln_g"], io["conv_ln_g"].ap.rearrange("(c p) -> c p", p=128)),
                        (io["conv_ln_b"], io["conv_ln_b"].ap.rearrange("(c p) -> c p", p=128)),
                        (io["b_pw"], io["b_pw"].ap.rearrange("(c p) -> c p", p=128))], "pp1b")
    DWB, LNG, LNB, BPW = 0, 8, 16, 24
    wT = P.sb("wT", [128, 8, 32])
    dwr = P.sb("dwr", [128, 1024])
    P.dma("sp", dwr[0:31, :], io["dw_w"].ap, [io["dw_w"]], [dwr], dwr)
    ptw = P.ps("ptw", 6, 0, 512)
    for c in range(8):
        E(P, "pe", "transpose", [dwr, C.identf], [ptw], out=ptw[:, c * 32:c * 32 + 31], in_=dwr[0:31, c * 128:(c + 1) * 128],
          identity=C.identf[0:31, 0:31])
    E(P, "dve", "tensor_copy", [ptw], [wT], out=wT[:, :, 0:31], in_=ptw[:, 0:256].rearrange("p (c j) -> p c j", c=8)[:, :, 0:31])
    diag = P.sb("diag", [128, 8 * 31, 128], BF16)
    n = 0
    for c in range(8):
        for j in range(31):
            if n % 2 == 0:
                E(P, "dve", "tensor_scalar", [C.ident, wT], [diag], out=diag[:, c * 31 + j, :], in0=C.ident[:, :], scalar1=wT[:, c, j:j + 1],
                  scalar2=None, op0=ALU.mult)
            else:
                E(P, "act", "activation", [C.ident, wT], [diag], out=diag[:, c * 31 + j, :], in_=C.ident[:, :], func=AF.Copy, scale=wT[:, c, j:j + 1])
            n += 1
    wpw = P.sb("wpw", [128, 8, 1024], BF16)
    WL = WLoader(P, nstage=2, words=2048)
    wpv = io["w_pw"].ap.rearrange("(k p) n -> p k n", p=128)
    for k in range(0, 8, 2):
        WL.load(wpw, wpw[:, k:k + 2, :], io["w_pw"], wpv[:, k:k + 2, :], [128, 2, 1024], engs=("pool",))
    tl = P.sb("tl", [128, 2, 8, 4, 32], BF16)
    P.dma("sp", tl[:, :, :, :, :].rearrange("p r c q t -> p r (c q t)"), io["tails"].ap.rearrange("(r p) n -> p r n", p=128),
          [io["tails"]], [tl], tl)
    tlf = P.sb("tlf", [128, 2, 8, 4, 32])
    E(P, "dve", "tensor_copy", [tl], [tlf], out=tlf[:, :, :, :, :].rearrange("p r c q t -> p (r c q t)"),
      in_=tl[:, :, :, :, :].rearrange("p r c q t -> p (r c q t)"))
    selb = P.sb("selb", [128, 32])
    P.dma("sp", selb[:, :], io["sel"].ap.partition_broadcast(128), [io["sel"]], [selb], selb)
    hacc = P.sb("hacc", [128, 8, 32])
    htmp = P.sb("htmp", [128, 8, 32])
    ub = [P.sb("ub%d" % i, [128, 8, 544], BF16) for i in range(2)]
    ustg = [P.sb("ustg%d" % i, [128, 8, 512], BF16) for i in range(1)]
    vT = P.sb("vT", [128, 8, 512])
    vb = [P.sb("vb%d" % i, [128, 512], BF16) for i in range(2)]
    sqb = [P.sb("sqb%d" % i, [128, 512], BF16) for i in range(2)]
    mean = P.sb("mean", [128, 512]); msq = P.sb("msq", [128, 512]); rstd = P.sb("rstd", [128, 512])
    t1 = [P.sb("t1_%d" % i, [128, 512]) for i in range(2)]
    act = P.sb("act", [128, 8, 512], BF16)
    sgb = [P.sb("sgb%d" % i, [128, 8, 512], BF16) for i in range(2)]
    cst = [P.sb("cst%d" % i, [128, 512], BF16) for i in range(2)]
    psc = [P.ps("psc%d" % i, i) for i in range(4)]
    pS = P.ps("pS", 4)
    pQ = P.ps("pQ", 5)
    pso = [P.ps("pso%d" % i, 6 + i) for i in range(2)]
    nv = 0
    no = 0
    if stop_after < 1:
        P.barrier()
        return
    for j in range(NQT):
        U = ub[j % 2]
        SG = sgb[j % 2]
        qs = slice(j * 512, (j + 1) * 512)
        US = ustg[0]
        P.dma("sp", US[:, :, :], io["uT"].ap[:, :, j, :].rearrange("c p t -> p c t"), [io["uT"]], [US], US)
        P.dma("sp", SG[:, :, :], io["sgc"].ap[:, :, qs].rearrange("c p t -> p c t"), [io["sgc"]], [SG], SG)
        if stop_after == 1:
            P.barrier()
            return
        for s in range(8):
            src = tlf[:, s // 4, :, s % 4, :]
            sb_ = bc(selb[:, j * 8 + s:j * 8 + s + 1].unsqueeze(2), [128, 8, 32])
            if s == 0:
                E(P, "dve", "tensor_tensor", [tlf, selb], [hacc], out=hacc[:, :, :], in0=src, in1=sb_, op=ALU.mult)
                if stop_after == 11:
                    P.barrier()
                    return
            else:
                E(P, "dve", "tensor_tensor", [tlf, selb], [htmp], out=htmp[:, :, :], in0=src, in1=sb_, op=ALU.mult)
                E(P, "dve", "tensor_tensor", [hacc, htmp], [hacc], out=hacc[:, :, :], in0=hacc[:, :, :], in1=htmp[:, :, :], op=ALU.add)
        if stop_after == 12:
            P.barrier()
            return
        HT = ub[1] if stop_after == 13 else U
        HO = 512 if stop_after == 14 else 0
        E(P, "dve", "tensor_copy", [hacc], [HT], out=HT[:, :, HO:HO + 32], in_=hacc[:, :, :])
        E(P, "pool", "tensor_copy", [US], [U], out=U[:, :, 32:544], in_=US[:, :, :])
        if stop_after < 2 or stop_after in (11, 12, 13):
            P.barrier()
            return
        for c in range(8):
            pc = psc[nv % 4]
            VB = vb[nv % 2]
            SQ = sqb[nv % 2]
            nv += 1
            for jj in range(31):
                E(P, "pe", "matmul", [diag, U], [pc], out=pc[:, :], lhsT=diag[:, c * 31 + jj, :], rhs=U[:, c, 2 + jj:2 + jj + 512],
                  start=(jj == 0), stop=(jj == 30))
            if stop_after == 20:
                P.barrier(); return
            E(P, "act", "activation", [pc, pp], [vT], out=vT[:, c, :], in_=pc[:, :], func=AF.Identity, bias=pp[:, DWB + c:DWB + c + 1])
            if stop_after == 21:
                P.barrier(); return
            E(P, "act", "activation", [vT], [SQ], out=SQ[:, :], in_=vT[:, c, :], func=AF.Square)
            if stop_after == 22:
                P.barrier(); return
            E(P, "dve", "tensor_copy", [vT], [VB], out=VB[:, :], in_=vT[:, c, :])
            if stop_after == 23:
                P.barrier(); return
            E(P, "pe", "matmul", [C.ones, VB], [pS], out=pS[:, :], lhsT=C.ones[:, :], rhs=VB[:, :], start=(c == 0), stop=(c == 7))
            E(P, "pe", "matmul", [C.ones, SQ], [pQ], out=pQ[:, :], lhsT=C.ones[:, :], rhs=SQ[:, :], start=(c == 0), stop=(c == 7))
        if stop_after < 3:
            P.barrier()
            return
        E(P, "dve", "tensor_scalar", [pS], [mean], out=mean[:, :], in0=pS[:, :], scalar1=1.0 / 1024, scalar2=None, op0=ALU.mult)
        E(P, "dve", "tensor_tensor", [mean], [msq], out=msq[:, :], in0=mean[:, :], in1=mean[:, :], op=ALU.mult)
        E(P, "dve", "scalar_tensor_tensor", [pQ, msq], [msq], out=msq[:, :], in0=pQ[:, :], scalar=1.0 / 1024, in1=msq[:, :],
          op0=ALU.mult, op1=ALU.subtract)
        E(P, "act", "activation", [msq, C.eps], [rstd], out=rstd[:, :], in_=msq[:, :], func=AF.Sqrt, bias=C.eps[:, 0:1])
        E(P, "dve", "reciprocal", [rstd], [rstd], out=rstd[:, :], in_=rstd[:, :])
        for c in range(8):
            T = t1[c % 2]
            E(P, "dve", "tensor_tensor", [vT, mean], [T], out=T[:, :], in0=vT[:, c, :], in1=mean[:, :], op=ALU.subtract)
            E(P, "dve", "tensor_tensor", [T, rstd], [T], out=T[:, :], in0=T[:, :], in1=rstd[:, :], op=ALU.mult)
            E(P, "act", "activation", [T, pp], [act], out=act[:, c, :], in_=T[:, :], func=AF.Silu, scale=pp[:, LNG + c:LNG + c + 1],
              bias=pp[:, LNB + c:LNB + c + 1])
        if stop_after < 4:
            P.barrier()
            return
        for co in range(8):
            po = pso[no % 2]
            CS = cst[no % 2]
            no += 1
            for ci in range(8):
                E(P, "pe", "matmul", [wpw, act], [po], out=po[:, :], lhsT=wpw[:, ci, co * 128:(co + 1) * 128], rhs=act[:, ci, :],
                  start=(ci == 0), stop=(ci == 7))
            E(P, "dve", "scalar_tensor_tensor", [po, pp, SG], [CS], out=CS[:, :], in0=po[:, :], scalar=pp[:, BPW + co:BPW + co + 1],
              in1=SG[:, co, :], op0=ALU.add, op1=ALU.mult)
            P.dma("sp", io["convT"].ap[co, :, qs], CS[:, :], [CS], [io["convT"]], CS)
    P.barrier()
    P.release(m0)


PH1B_IN = [("uT", [8, 128, 4, 512], BF16), ("tails", [256, 1024], BF16), ("sel", [32], F32), ("dw_w", [31, 1024], F32),
           ("dw_b", [1024], F32), ("conv_ln_g", [1024], F32), ("conv_ln_b", [1024], F32), ("w_pw", [1024, 1024], F32),
           ("b_pw", [1024], F32), ("sgc", [8, 128, NT], BF16)]
PH1B_OUT = [("convT", [8, 128, NT], BF16)]


def halo_sel(core):
    sel = np.zeros(32, np.float32)
    for j, g in enumerate(core_tiles(core)):
        if g > 0:
            rk, loc = gtile_src(g - 1)
            sel[j * 8 + rk * 4 + loc] = 1.0
    return sel


def emit_ph0(P, C, io):
    m0 = P.mark()
    ct = P.sb("ct", [128, 64])
    P.dma("sp", ct[:, :], io["cT"].ap, [io["cT"]], [ct], ct)
    E(P, "act", "activation", [ct], [ct], out=ct[:, :], in_=ct[:, :], func=AF.Silu)
    bias = P.sb("adab", [128, 2 * 768])
    P.dma("sp", bias[0:4, :], io["ada_bs"].ap.rearrange("l n -> (l n)").partition_broadcast(4), [io["ada_bs"]], [bias], bias)
    wst = [P.sb("adaw%d" % i, [128, 4, 384]) for i in range(3)]
    res = P.sb("adares", [128, 2 * 768])
    pss = [P.ps("ph0ps%d" % i, i) for i in range(4)]
    ctv = ct[:, :].rearrange("p (k b) -> p k b", b=4)
    n = 0
    for l in range(2):
        wv = io["ada_ws"].ap[l].rearrange("(k p) n -> p k n", p=128)
        for hf in range(2):
            ps = pss[l * 2 + hf]
            for kg in range(4):
                W = wst[n % 3]
                n += 1
                P.dma("sp", W[:, :, :], wv[:, kg * 4:(kg + 1) * 4, hf * 384:(hf + 1) * 384], [io["ada_ws"]], [W], W)
                for kk in range(4):
                    k = kg * 4 + kk
                    E(P, "pe", "matmul", [ct, W], [ps], out=ps[0:4, 0:384], lhsT=ctv[:, k, :], rhs=W[:, kk, :], start=(k == 0), stop=(k == 15))
            o = (l * 2 + hf) * 384
            E(P, "dve", "tensor_tensor", [ps, bias], [res], out=res[0:4, o:o + 384], in0=ps[0:4, 0:384], in1=bias[0:4, o:o + 384], op=ALU.add)
    P.dma("sp", io["mods"].ap.rearrange("b l n -> b (l n)"), res[0:4, :], [res], [io["mods"]], res)
    P.barrier()
    P.release(m0)


PH0_IN = [("cT", [128, 64], F32), ("ada_ws", [2, D, 768], F32), ("ada_bs", [2, 768], F32)]
PH0_OUT = [("mods", [4, 2, 768], F32)]

PHB_IN = [("uT", [8, 128, 4, 512], BF16), ("tails", [256, 1024], BF16), ("sel", [32], F32), ("dw_w", [31, 1024], F32),
          ("dw_b", [1024], F32), ("conv_ln_g", [1024], F32), ("conv_ln_b", [1024], F32), ("w_pw", [1024, 1024], F32),
          ("b_pw", [1024], F32), ("sgc", [8, 128, NT], BF16), ("kvall", [2 * KVROWS, 2048], BF16), ("qTn", [8, 128, NT], BF16),
          ("qTr", [4, 128, NT], BF16), ("sgm", [8, 128, NT], BF16), ("qpos", [NT], F32), ("x", [NT, D], F32),
          ("mod", [6144], F32), ("w_out", [D, D], F32)]
PHB_OUT = [("xnew", [NT, D], F32)]


def emit_back(P, C, io):
    emit_ph1b(P, C, io)
    emit_ph2(P, C, io)
    emit_ph3(P, C, io)


def build_back():
    nc = bass.Bass("TRN2", target_bir_lowering=False)
    P = Prog(nc)
    io = {}
    for n, sh, dt in PHB_IN:
        io[n] = P.dram(n, sh, dt, kind="ExternalInput")
    for n, sh, dt in PHB_OUT:
        io[n] = P.dram(n, sh, dt, kind="ExternalOutput")
    io["convT"] = P.dram("convT", [8, 128, NT], BF16)
    io["mlaT"] = P.dram("mlaT", [8, 128, NT], BF16)
    C = emit_consts(P)
    emit_back(P, C, io)
    P.barrier()
    P.finalize()
    return nc


def _c(a, dt=None):
    a = np.ascontiguousarray(a)
    return a if dt is None else a.astype(dt)


def kernel_unfused(x, c, positions, ada_w, ada_b, norm_g, w_in, q_lat_g, w_q_up, kv_lat_g, w_kv_up, q_norm_g, k_norm_g,
                   glu_b, dw_w, dw_b, conv_ln_g, conv_ln_b, w_pw, b_pw, w_out):
    cores = list(range(8))
    toks = [local_tokens(cr) for cr in cores]
    cT = _c(np.asarray(c, np.float32).reshape(4, 16, 128).transpose(2, 1, 0).reshape(128, 64))
    ins = [dict(cT=cT, ada_ws=_c(ada_w[:, :, i * 768:(i + 1) * 768]), ada_bs=_c(ada_b[:, i * 768:(i + 1) * 768])) for i in cores]
    r0 = run_bass_kernel_spmd(build_phase(emit_ph0, PH0_IN, PH0_OUT), ins, core_ids=cores)
    mods = np.concatenate([np.asarray(r0.results[i]["mods"]) for i in cores], axis=2)
    xl = [_c(x[cr // 2][toks[cr]]) for cr in cores]
    nc_a = build_ph1a()
    nc_b = build_back()
    for l in range(2):
        ins = []
        for cr in cores:
            b = cr // 2
            ins.append(dict(x=xl[cr], mod=_c(mods[b, l]), norm_g=_c(norm_g[l]), glu_b=_c(glu_b[l]), q_lat_g=_c(q_lat_g[l]),
                            kv_lat_g=_c(kv_lat_g[l]), pos=_c(positions[b][toks[cr]], np.int32), invf=INVF, q_norm_g=_c(q_norm_g[l]),
                            k_norm_g=_c(k_norm_g[l]), w_in=_c(w_in[l]), w_q_up=_c(w_q_up[l]), w_kv_up=_c(w_kv_up[l])))
        ra = run_bass_kernel_spmd(nc_a, ins, core_ids=cores).results
        ins = []
        for cr in cores:
            b = cr // 2
            kvall = kv_exchange_host(np.asarray(ra[2 * b]["kvsend"]), np.asarray(ra[2 * b + 1]["kvsend"]))
            tails = np.concatenate([np.asarray(ra[2 * b]["utail"]), np.asarray(ra[2 * b + 1]["utail"])], axis=0)
            ins.append(dict(uT=np.asarray(ra[cr]["uT"]), tails=tails, sel=halo_sel(cr), dw_w=_c(dw_w[l]), dw_b=_c(dw_b[l]),
                            conv_ln_g=_c(conv_ln_g[l]), conv_ln_b=_c(conv_ln_b[l]), w_pw=_c(w_pw[l]), b_pw=_c(b_pw[l]),
                            sgc=np.asarray(ra[cr]["sgc"]), kvall=kvall, qTn=np.asarray(ra[cr]["qTn"]), qTr=np.asarray(ra[cr]["qTr"]),
                            sgm=np.asarray(ra[cr]["sgm"]), qpos=toks[cr].astype(np.float32), x=xl[cr], mod=_c(mods[b, l]),
                            w_out=_c(w_out[l])))
        rb = run_bass_kernel_spmd(nc_b, ins, core_ids=cores).results
        xl = [np.asarray(rb[cr]["xnew"]) for cr in cores]
    out = np.empty((4, 4096, D), np.float32)
    for cr in cores:
        out[cr // 2][toks[cr]] = xl[cr]
    return out


PAIRS = [[0, 1], [2, 3], [4, 5], [6, 7]]
W_NAMES = [("norm_g", [D]), ("w_in", [D, 4928]), ("q_lat_g", [512]), ("w_q_up", [512, 1536]), ("kv_lat_g", [256]),
           ("w_kv_up", [256, 2048]), ("q_norm_g", [192]), ("k_norm_g", [192]), ("glu_b", [2048]), ("dw_w", [31, 1024]),
           ("dw_b", [1024]), ("conv_ln_g", [1024]), ("conv_ln_b", [1024]), ("w_pw", [1024, 1024]), ("b_pw", [1024]),
           ("w_out", [D, D])]


def emit_modsel(P, C, io):
    m0 = P.mark()
    M4 = P.sb("M4", [128, 8, 1536])
    P.dma("sp", M4[0:4, :, :], io["modall"].ap.rearrange("(r b) n -> b r n", b=4), [io["modall"]], [M4], M4)
    s4 = P.sb("s4", [128, 1])
    P.dma("sp", s4[0:4, :], io["sel4"].ap.rearrange("(b o) -> b o", o=1), [io["sel4"]], [s4], s4)
    mo = P.sb("mo", [128, 2 * 6144])
    pss = [P.ps("msps%d" % i, i) for i in range(8)]
    n = 0
    for l in range(2):
        for r in range(8):
            for hf in range(2):
                ps = pss[n % 8]
                n += 1
                E(P, "pe", "matmul", [s4, M4], [ps], out=ps[0:1, 0:384], lhsT=s4[0:4, 0:1], rhs=M4[0:4, r, l * 768 + hf * 384: l * 768 + (hf + 1) * 384],
                  start=True, stop=True)
                o = l * 6144 + r * 768 + hf * 384
                E(P, "dve", "tensor_copy", [ps], [mo], out=mo[0:1, o:o + 384], in_=ps[0:1, 0:384])
    P.dma("sp", io["modsel"].ap.rearrange("(o l) n -> o (l n)", o=1), mo[0:1, :], [mo], [io["modsel"]], mo)
    P.barrier()
    P.release(m0)


BLOB = [("x", [NT, D]), ("w_in", [2, D, 4928]), ("w_q_up", [2, 512, 1536]), ("w_kv_up", [2, 256, 2048]),
        ("w_pw", [2, 1024, 1024]), ("w_out", [2, D, D])]
BLOB_OFF = {}
_o = 0
for _n, _sh in BLOB:
    BLOB_OFF[_n] = _o
    _o += int(np.prod(_sh))
BLOB_N = _o


def build_fused():
    nc = bass.Bass("TRN2", target_bir_lowering=False)
    P = Prog(nc)
    ext = {}
    blob = P.dram("blob", [BLOB_N], F32, kind="ExternalInput")
    for n, sh in BLOB:
        v = blob.ap[BLOB_OFF[n]:BLOB_OFF[n] + int(np.prod(sh))]
        if len(sh) == 2:
            v = v.rearrange("(a b) -> a b", a=sh[0])
        else:
            v = v.rearrange("(l a b) -> l a b", l=sh[0], a=sh[1])
        ext[n] = Tile(n, v)
    for n, sh, dt in [("cT", [128, 64], F32), ("ada_ws", [2, D, 768], F32), ("ada_bs", [2, 768], F32),
                      ("pos", [NT], I32), ("invf", [32], F32), ("qpos", [NT], F32), ("sel", [32], F32), ("sel4", [4], F32)]:
        ext[n] = P.dram(n, sh, dt, kind="ExternalInput")
    for n, sh in W_NAMES:
        if n not in BLOB_OFF:
            ext[n] = P.dram(n, [2] + sh, F32, kind="ExternalInput")
    out = P.dram("out", [NT, D], F32, kind="ExternalOutput")
    d = {}
    for n, sh, dt in [("mods", [4, 2, 768], F32), ("modall", [32, 1536], F32), ("modsel", [2, 6144], F32),
                      ("kvsend", [KVROWS, 2048], BF16), ("kvall", [2 * KVROWS, 2048], BF16), ("qTn", [8, 128, NT], BF16),
                      ("qTr", [4, 128, NT], BF16), ("uT", [8, 128, 4, 512], BF16), ("utail", [128, 1024], BF16),
                      ("tails", [256, 1024], BF16), ("sgm", [8, 128, NT], BF16), ("sgc", [8, 128, NT], BF16),
                      ("convT", [8, 128, NT], BF16), ("mlaT", [8, 128, NT], BF16), ("x1", [NT, D], F32)]:
        d[n] = P.dram(n, sh, dt)
    C = emit_consts(P)
    semt = [P.sb("ccsem%d" % i, [128, 8]) for i in range(3)]
    emit_ph0(P, C, dict(cT=ext["cT"], ada_ws=ext["ada_ws"], ada_bs=ext["ada_bs"], mods=d["mods"]))
    P.allgather([list(range(8))], d["mods"], d["modall"], semt[0])
    P.barrier()
    emit_modsel(P, C, dict(modall=d["modall"], sel4=ext["sel4"], modsel=d["modsel"]))
    for l in range(2):
        io = dict(d)
        for n, sh in W_NAMES:
            io[n] = Tile(n + str(l), ext[n].ap[l])
        io["mod"] = Tile("mod%d" % l, d["modsel"].ap[l])
        io["x"] = ext["x"] if l == 0 else d["x1"]
        io["xnew"] = d["x1"] if l == 0 else out
        for n in ("pos", "invf", "qpos", "sel"):
            io[n] = ext[n]
        emit_ph1a(P, C, io)
        P.allgather(PAIRS, d["utail"], d["tails"], semt[1])
        for i in range(KVROWS // 256):
            P.allgather(PAIRS, d["kvsend"], d["kvall"], semt[2], src_ap=d["kvsend"].ap[i * 256:(i + 1) * 256, :],
                        dst_ap=d["kvall"].ap[i * 512:(i + 1) * 512, :])
        emit_ph1b(P, C, io)
        emit_ph2(P, C, io)
        emit_ph3(P, C, io)
    P.barrier()
    P.finalize()
    return nc


_NC_CACHE = {}


def kernel_fused(x, c, positions, ada_w, ada_b, **w):
    cores = list(range(8))
    toks = [local_tokens(cr) for cr in cores]
    cT = _c(np.asarray(c, np.float32).reshape(4, 16, 128).transpose(2, 1, 0).reshape(128, 64))
    ins = []
    for cr in cores:
        b = cr // 2
        s4 = np.zeros(4, np.float32)
        s4[b] = 1.0
        blob = np.empty(BLOB_N, np.float32)
        blob[0:NT * D] = np.asarray(x[b], np.float32)[toks[cr]].reshape(-1)
        for n, sh in BLOB[1:]:
            blob[BLOB_OFF[n]:BLOB_OFF[n] + int(np.prod(sh))] = np.asarray(w[n], np.float32).reshape(-1)
        dd = dict(blob=blob, cT=cT, ada_ws=_c(ada_w[:, :, cr * 768:(cr + 1) * 768]), ada_bs=_c(ada_b[:, cr * 768:(cr + 1) * 768]),
                  pos=_c(positions[b][toks[cr]], np.int32), invf=INVF, qpos=toks[cr].astype(np.float32), sel=halo_sel(cr), sel4=s4)
        for n, sh in W_NAMES:
            if n not in BLOB_OFF:
                dd[n] = _c(w[n], np.float32)
        ins.append(dd)
    if "fused" not in _NC_CACHE:
        _NC_CACHE["fused"] = build_fused()
    res = run_bass_kernel_spmd(_NC_CACHE["fused"], ins, core_ids=cores).results
    out = np.empty((4, 4096, D), np.float32)
    for cr in cores:
        out[cr // 2][toks[cr]] = np.asarray(res[cr]["out"])
    return out


def kernel(**inputs):
    inputs = {k: np.asarray(v) for k, v in inputs.items()}
    return kernel_fused(**inputs)
```
